# Optimizing a Trainium2 kernel written in Bass

```python
import math
import jax, jax.numpy as jnp
from jax import lax
import numpy as np

D_MODEL = 1024
BATCH = 8
SEQ = 2048
DEPTH = 2
DEC_BATCH = 32
DEC_SEQ = 32
PAST_LEN = 1024

CHUNK = 64
LEFT_CHUNKS = 8
BAND = (LEFT_CHUNKS + 1) * CHUNK
Q_BLOCK = 128
HEAD_DIM = 64
A_HEADS = D_MODEL // (2 * HEAD_DIM)
B_HEADS = D_MODEL // HEAD_DIM
D_FF = 4 * D_MODEL
REL_CLIP = 128
ROPE_THETA = 10000.0
EPS = 1e-6
NEG_INF = -1e30
N_A = DEPTH // 2
N_B = DEPTH - N_A

kernel_name = "yoco_diffattn_chunkband_stream_step"


def rmsnorm(x, g):
    xf = x.astype(jnp.float32)
    y = xf * lax.rsqrt(jnp.mean(xf * xf, axis=-1, keepdims=True) + EPS)
    return (y * g.astype(jnp.float32)).astype(x.dtype)


def rope(x, pos):
    half = HEAD_DIM // 2
    inv = 1.0 / (ROPE_THETA ** (jnp.arange(half, dtype=jnp.float32) / half))
    ang = pos.astype(jnp.float32)[:, None] * inv[None, :]
    cos = jnp.cos(ang)[:, None, None, :]
    sin = jnp.sin(ang)[:, None, None, :]
    xf = x.astype(jnp.float32)
    x1, x2 = xf[..., :half], xf[..., half:]
    return jnp.concatenate([x1 * cos - x2 * sin, x2 * cos + x1 * sin], axis=-1).astype(x.dtype)


def lambda_init(layer):
    return 0.8 - 0.6 * math.exp(-0.3 * layer)


def diff_lambda(lp, lam0):
    lp = lp.astype(jnp.float32)
    return jnp.exp(jnp.sum(lp[0] * lp[1])) - jnp.exp(jnp.sum(lp[2] * lp[3])) + lam0


def diff_weights(q, k, lam, mask):
    s = jnp.einsum("bqhcd,bkhcd->bhcqk", q, k).astype(jnp.float32) * (HEAD_DIM ** -0.5)
    if mask is not None:
        s = jnp.where(mask, s, NEG_INF)
    p = jax.nn.softmax(s, axis=-1)
    return p[:, :, 0] - lam * p[:, :, 1]


def diff_attn_prompt(q, k, v, lam):
    nb, ns = q.shape[0], q.shape[1]
    nblk = ns // Q_BLOCK
    qb = q.reshape(nb, nblk, Q_BLOCK, A_HEADS, 2, HEAD_DIM).transpose(1, 0, 2, 3, 4, 5)
    k_chunk = jnp.arange(ns) // CHUNK

    def one(args):
        qi, bi = args
        q_chunk = (bi * Q_BLOCK + jnp.arange(Q_BLOCK)) // CHUNK
        mask = q_chunk[:, None] >= k_chunk[None, :]
        p = diff_weights(qi, k, lam, mask)
        return jnp.einsum("bhqk,bkhe->bqhe", p.astype(v.dtype), v)

    o = lax.map(one, (qb, jnp.arange(nblk)))
    return o.transpose(1, 0, 2, 3, 4).reshape(nb, ns, A_HEADS, 2 * HEAD_DIM)


def diff_attn_full(q, k, v, lam):
    p = diff_weights(q, k, lam, None)
    return jnp.einsum("bhqk,bkhe->bqhe", p.astype(v.dtype), v)


def rel_bias(table, rel):
    idx = jnp.clip(rel, -REL_CLIP, REL_CLIP) + REL_CLIP
    return table.astype(jnp.float32)[:, idx]


def band_attn_prompt(q, k, v, table):
    nb, ns = q.shape[0], q.shape[1]
    nc = ns // CHUNK
    pad = LEFT_CHUNKS * CHUNK
    kp = jnp.pad(k, ((0, 0), (pad, 0), (0, 0), (0, 0)))
    vp = jnp.pad(v, ((0, 0), (pad, 0), (0, 0), (0, 0)))
    qc = q.reshape(nb, nc, CHUNK, B_HEADS, HEAD_DIM).transpose(1, 0, 2, 3, 4)
    rel = pad + jnp.arange(CHUNK)[:, None] - jnp.arange(BAND)[None, :]
    bias = rel_bias(table, rel)[None]

    def one(args):
        qi, ci = args
        kb = lax.dynamic_slice_in_dim(kp, ci * CHUNK, BAND, axis=1)
        vb = lax.dynamic_slice_in_dim(vp, ci * CHUNK, BAND, axis=1)
        valid = jnp.arange(BAND) >= (LEFT_CHUNKS - ci) * CHUNK
        s = jnp.einsum("bqhd,bkhd->bhqk", qi, kb).astype(jnp.float32) * (HEAD_DIM ** -0.5) + bias
        p = jax.nn.softmax(jnp.where(valid, s, NEG_INF), axis=-1)
        return jnp.einsum("bhqk,bkhd->bqhd", p.astype(vb.dtype), vb)

    o = lax.map(one, (qc, jnp.arange(nc)))
    return o.transpose(1, 0, 2, 3, 4).reshape(nb, ns, B_HEADS * HEAD_DIM)


def band_attn_sample(q, k_new, v_new, cache_k, cache_v, table, past):
    nb, nt = q.shape[0], q.shape[1]
    lb = cache_k.shape[1]
    k_all = jnp.concatenate([cache_k, k_new], axis=1)
    v_all = jnp.concatenate([cache_v, v_new], axis=1)
    qpos = jnp.arange(past, past + nt)
    kpos = jnp.arange(past - lb, past + nt)
    bias = rel_bias(table, qpos[:, None] - kpos[None, :])[None]
    s = jnp.einsum("bqhd,bkhd->bhqk", q, k_all).astype(jnp.float32) * (HEAD_DIM ** -0.5) + bias
    p = jax.nn.softmax(s, axis=-1)
    return jnp.einsum("bhqk,bkhd->bqhd", p.astype(v_all.dtype), v_all).reshape(nb, nt, B_HEADS * HEAD_DIM)


def sq_relu_mlp(h, w1, w2):
    return jnp.square(jax.nn.relu(h @ w1)) @ w2


def _trunk(x, pos, cache_a_k, cache_a_v, cache_b_k, cache_b_v,
           g_attn, w_a_qkv, a_lambda, a_subln, w_a_o, g_kv, w_kv, w_b_q, b_rel, w_b_o,
           g_mlp, w_ff1, w_ff2, g_final):
    prompt = cache_a_k is None
    nb, ns = x.shape[0], x.shape[1]
    h = x
    new_ak, new_av = [], []
    k_sh = v_sh = None
    for l in range(DEPTH):
        if l < N_A:
            hn = rmsnorm(h, g_attn[l])
            q, k, v = jnp.split(hn @ w_a_qkv[l], 3, axis=-1)
            q = rope(q.reshape(nb, ns, A_HEADS, 2, HEAD_DIM), pos)
            k = rope(k.reshape(nb, ns, A_HEADS, 2, HEAD_DIM), pos)
            v = v.reshape(nb, ns, A_HEADS, 2 * HEAD_DIM)
            new_ak.append(k.reshape(nb, ns, A_HEADS, 2 * HEAD_DIM))
            new_av.append(v)
            lam0 = lambda_init(l)
            lam = diff_lambda(a_lambda[l], lam0)
            if prompt:
                o = diff_attn_prompt(q, k, v, lam)
            else:
                past = cache_a_k.shape[2]
                k_all = jnp.concatenate(
                    [cache_a_k[l].reshape(nb, past, A_HEADS, 2, HEAD_DIM), k], axis=1)
                v_all = jnp.concatenate([cache_a_v[l], v], axis=1)
                o = diff_attn_full(q, k_all, v_all, lam)
            o = rmsnorm(o, a_subln[l]) * (1.0 - lam0)
            h = h + o.reshape(nb, ns, D_MODEL) @ w_a_o[l]
        else:
            j = l - N_A
            if j == 0:
                kv = rmsnorm(h, g_kv) @ w_kv
                k_sh, v_sh = jnp.split(kv, 2, axis=-1)
                k_sh = k_sh.reshape(nb, ns, B_HEADS, HEAD_DIM)
                v_sh = v_sh.reshape(nb, ns, B_HEADS, HEAD_DIM)
            q = (rmsnorm(h, g_attn[l]) @ w_b_q[j]).reshape(nb, ns, B_HEADS, HEAD_DIM)
            if prompt:
                o = band_attn_prompt(q, k_sh, v_sh, b_rel[j])
            else:
                o = band_attn_sample(q, k_sh, v_sh, cache_b_k, cache_b_v, b_rel[j], cache_a_k.shape[2])
            h = h + o @ w_b_o[j]
        h = h + sq_relu_mlp(rmsnorm(h, g_mlp[l]), w_ff1[l], w_ff2[l])
    y = rmsnorm(h, g_final)
    if prompt:
        keep = min(LEFT_CHUNKS * CHUNK, ns)
        k_sh, v_sh = k_sh[:, ns - keep:], v_sh[:, ns - keep:]
    return y, jnp.stack(new_ak), jnp.stack(new_av), k_sh, v_sh


def setup_inputs(seed: int = 0) -> dict:
    key = jax.random.key(seed)
    ks = jax.random.split(key, 20)
    f32 = jnp.float32
    lb = min(LEFT_CHUNKS * CHUNK, PAST_LEN)

    def nrm(k, shape, scale):
        return jax.random.normal(k, shape, f32) * scale

    return {
        "x_prompt": nrm(ks[0], (BATCH, SEQ, D_MODEL), 1.0),
        "x_sample": nrm(ks[1], (DEC_BATCH, DEC_SEQ, D_MODEL), 1.0),
        "cache_a_k": nrm(ks[2], (N_A, DEC_BATCH, PAST_LEN, A_HEADS, 2 * HEAD_DIM), 1.0),
        "cache_a_v": nrm(ks[3], (N_A, DEC_BATCH, PAST_LEN, A_HEADS, 2 * HEAD_DIM), 1.0),
        "cache_b_k": nrm(ks[4], (DEC_BATCH, lb, B_HEADS, HEAD_DIM), 1.0),
        "cache_b_v": nrm(ks[5], (DEC_BATCH, lb, B_HEADS, HEAD_DIM), 1.0),
        "g_attn": 1.0 + nrm(ks[6], (DEPTH, D_MODEL), 0.02),
        "w_a_qkv": nrm(ks[7], (N_A, D_MODEL, 3 * D_MODEL), D_MODEL ** -0.5),
        "a_lambda": nrm(ks[8], (N_A, 4, HEAD_DIM), 0.1),
        "a_subln": 1.0 + nrm(ks[9], (N_A, 2 * HEAD_DIM), 0.02),
        "w_a_o": nrm(ks[10], (N_A, D_MODEL, D_MODEL), D_MODEL ** -0.5),
        "g_kv": 1.0 + nrm(ks[11], (D_MODEL,), 0.02),
        "w_kv": nrm(ks[12], (D_MODEL, 2 * D_MODEL), D_MODEL ** -0.5),
        "w_b_q": nrm(ks[13], (N_B, D_MODEL, D_MODEL), D_MODEL ** -0.5),
        "b_rel": nrm(ks[14], (N_B, B_HEADS, 2 * REL_CLIP + 1), 0.5),
        "w_b_o": nrm(ks[15], (N_B, D_MODEL, D_MODEL), D_MODEL ** -0.5),
        "g_mlp": 1.0 + nrm(ks[16], (DEPTH, D_MODEL), 0.02),
        "w_ff1": nrm(ks[17], (DEPTH, D_MODEL, D_FF), D_MODEL ** -0.5),
        "w_ff2": nrm(ks[18], (DEPTH, D_FF, D_MODEL), 0.5 * D_FF ** -0.5),
        "g_final": 1.0 + nrm(ks[19], (D_MODEL,), 0.02),
    }


def reference(x_prompt, x_sample, cache_a_k, cache_a_v, cache_b_k, cache_b_v,
              g_attn, w_a_qkv, a_lambda, a_subln, w_a_o, g_kv, w_kv, w_b_q, b_rel, w_b_o,
              g_mlp, w_ff1, w_ff2, g_final):
    seq = x_prompt.shape[1]
    past = cache_a_k.shape[2]
    nt = x_sample.shape[1]
    y_prompt, ak_p, av_p, bk_p, bv_p = _trunk(
        x_prompt, jnp.arange(seq), None, None, None, None,
        g_attn, w_a_qkv, a_lambda, a_subln, w_a_o, g_kv, w_kv, w_b_q, b_rel, w_b_o,
        g_mlp, w_ff1, w_ff2, g_final)
    y_sample, ak_s, av_s, bk_s, bv_s = _trunk(
        x_sample, jnp.arange(past, past + nt), cache_a_k, cache_a_v, cache_b_k, cache_b_v,
        g_attn, w_a_qkv, a_lambda, a_subln, w_a_o, g_kv, w_kv, w_b_q, b_rel, w_b_o,
        g_mlp, w_ff1, w_ff2, g_final)
    return (y_prompt, y_sample, ak_p, av_p, bk_p, bv_p, ak_s, av_s, bk_s, bv_s)
```

```python
import math
from contextlib import ExitStack

import numpy as np

import concourse.bass as bass
import concourse.mybir as mybir
from concourse.bass_utils import run_bass_kernel_spmd

F32 = mybir.dt.float32
BF16 = mybir.dt.bfloat16
AF = mybir.ActivationFunctionType
ALU = mybir.AluOpType
AX = mybir.AxisListType

D = 1024
SEQ = 2048
NT = 17
NTOK = NT * 128
PAST = 1024
DFF = 4096
EPS = 1e-6
LEXT = 896
GW = 768
NEG = -30000.0
N_CORES = 8


class Sched:
    ENG = ("pe", "act", "dve", "pool", "sp")

    def __init__(self, nc, es, ndma=24):
        self.nc = nc
        self.ops = {e: [] for e in self.ENG}
        self.sem = {e: es.enter_context(nc.semaphore("s_" + e)) for e in self.ENG}
        self.cnt = {e: 0 for e in self.ENG}
        self.dsem = [es.enter_context(nc.semaphore("d%d" % i)) for i in range(ndma)]
        self.dcnt = [0] * ndma
        self.dnext2 = [0, 0]
        self.waited = {e: {} for e in self.ENG}
        self.lastw = {}
        self.readers = {}
        self.excl = {}

    def _deps(self, eng, reads, writes, excl):
        deps = {}

        def add(k, v):
            if k == "pe" and eng == "pe":
                return
            if deps.get(k, 0) < v:
                deps[k] = v
        for b in reads:
            ev = self.lastw.get(b)
            if ev is not None:
                add(*ev)
        for b in writes:
            ev = self.lastw.get(b)
            if ev is not None:
                add(*ev)
            for k, v in self.readers.get(b, {}).items():
                add(k, v)
        for b in excl:
            for k, v in self.excl.get(b, {}).items():
                if k != eng:
                    add(k, v)
        out = []
        w = self.waited[eng]
        for k, v in deps.items():
            if w.get(k, 0) < v:
                w[k] = v
                out.append((k, v))
        return out

    def _semof(self, k):
        return self.sem[k] if isinstance(k, str) else self.dsem[k]

    def _mark(self, ev, reads, writes, excl):
        k, v = ev
        for b in reads:
            r = self.readers.setdefault(b, {})
            if r.get(k, 0) < v:
                r[k] = v
        for b in writes:
            self.lastw[b] = ev
            self.readers[b] = {}
        for b in excl:
            self.excl[b] = {k: v}

    def op(self, eng, fn, reads=(), writes=(), excl=()):
        waits = self._deps(eng, reads, writes, excl)
        self.cnt[eng] += 1
        ev = (eng, self.cnt[eng])
        sem = self.sem[eng]
        waitl = [(self._semof(k), v) for k, v in waits]

        def emit(e):
            for s, v in waitl:
                e.wait_ge(s, v)
            fn(e).then_inc(sem, 1)
        self.ops[eng].append(emit)
        self._mark(ev, reads, writes, excl)

    def dma(self, eng, out, in_, reads=(), writes=()):
        half = len(self.dsem) // 2
        qi = 0 if eng == "pool" else 1
        k = qi * half + self.dnext2[qi]
        self.dnext2[qi] = (self.dnext2[qi] + 1) % half
        waits = self._deps(eng, reads, writes, ())
        prev = self.dcnt[k]
        if prev and self.waited[eng].get(k, 0) < prev:
            self.waited[eng][k] = prev
            waits.append((k, prev))
        self.dcnt[k] += 16
        ev = (k, self.dcnt[k])
        sem = self.dsem[k]
        waitl = [(self._semof(kk), v) for kk, v in waits]

        def emit(e):
            for s, v in waitl:
                e.wait_ge(s, v)
            e.dma_start(out=out, in_=in_).then_inc(sem, 16)
        self.ops[eng].append(emit)
        self._mark(ev, reads, writes, ())

    def barrier(self):
        waitl = []
        for e in self.ENG:
            if self.cnt[e]:
                waitl.append((e, self.cnt[e]))
        for k in range(len(self.dsem)):
            if self.dcnt[k]:
                waitl.append((k, self.dcnt[k]))
        for eng in self.ENG:
            mine = []
            for k, v in waitl:
                if self.waited[eng].get(k, 0) < v:
                    self.waited[eng][k] = v
                    mine.append((self._semof(k), v))

            def emit(e, mine=mine):
                for s, v in mine:
                    e.wait_ge(s, v)
            self.ops[eng].append(emit)
        self.lastw.clear()
        self.readers.clear()
        self.excl.clear()

    def end_phase(self):
        self.barrier()
        self.replay()

    def replay(self):
        ops = self.ops
        self.ops = {e: [] for e in self.ENG}
        with self.nc.Block() as block:
            @block.tensor
            def _(e):
                for f in ops["pe"]:
                    f(e)

            @block.scalar
            def _(e):
                for f in ops["act"]:
                    f(e)

            @block.vector
            def _(e):
                for f in ops["dve"]:
                    f(e)

            @block.gpsimd
            def _(e):
                for f in ops["pool"]:
                    f(e)

            @block.sync
            def _(e):
                for f in ops["sp"]:
                    f(e)


def build_nc():
    nc = bass.Bass("TRN2", target_bir_lowering=False)

    def din(name, shape):
        return nc.dram_tensor(name, list(shape), F32, kind="ExternalInput").ap()

    def dout(name, shape):
        return nc.dram_tensor(name, list(shape), F32, kind="ExternalOutput").ap()

    x_d = din("x", [NTOK, D])
    cak_d = din("cak", [4, PAST, D])
    cav_d = din("cav", [4, PAST, D])
    cbk_d = din("cbk", [4, 512, D])
    cbv_d = din("cbv", [4, 512, D])
    g_attn_d = din("g_attn", [2, D])
    wqkv_d = din("w_a_qkv", [D, 3 * D])
    alam_d = din("a_lambda", [256])
    asub_d = din("a_subln", [128])
    wao_d = din("w_a_o", [D, D])
    g_kv_d = din("g_kv", [D])
    wkv_d = din("w_kv", [D, 2 * D])
    wbq_d = din("w_b_q", [D, D])
    brel_d = din("b_rel", [16, 257])
    wbo_d = din("w_b_o", [D, D])
    g_mlp_d = din("g_mlp", [2, D])
    wff1_d = din("w_ff1", [2, D, DFF])
    wff2_d = din("w_ff2", [2, DFF, D])
    g_fin_d = din("g_final", [D])
    cs_d = din("c_cs", [128, NT, 64])
    css_d = din("c_css", [32, 64])
    ident_d = din("c_ident", [128, 128])
    mask_d = din("c_mask", [128, GW])
    epat_d = din("c_epat", [128, 2, 128])

    y_d = dout("y", [NTOK, D])
    ak_d = dout("ak", [NTOK, D])
    av_d = dout("av", [NTOK, D])
    bk_d = dout("bk", [640, D])
    bv_d = dout("bv", [640, D])

    rep_d = nc.dram_tensor("rep_scr", [16 * 128 * LEXT], F32).ap()

    with ExitStack() as es:
        S = Sched(nc, es)

        uniq = [0]

        def sbuf(stack, name, shape, dt):
            uniq[0] += 1
            return stack.enter_context(nc.sbuf_tensor("%s_%d" % (name, uniq[0]), list(shape), dt))

        def psum(name, shape, dt):
            return es.enter_context(nc.psum_tensor(name, list(shape), dt))

        h = sbuf(es, "h", [128, NT, D], F32)
        A = sbuf(es, "A", [128, 8, NTOK], BF16)
        cs = sbuf(es, "cs", [128, NT, 64], F32)
        css = sbuf(es, "css", [32, 64], F32)
        ident = sbuf(es, "ident", [128, 128], BF16)
        ones = sbuf(es, "ones", [128, 128], BF16)
        onesf = sbuf(es, "onesf", [128, 128], F32)
        epat = sbuf(es, "epat", [128, 2, 128], BF16)
        ss = sbuf(es, "ss", [128, NT], F32)
        rstd = sbuf(es, "rstd", [128, NT], F32)
        lamb = sbuf(es, "lamb", [128, 256], F32)
        lamt = sbuf(es, "lamt", [128, 128], F32)
        lams = sbuf(es, "lams", [128, 4], F32)
        neglam = sbuf(es, "neglam", [128, 1], F32)
        gsub = sbuf(es, "gsub", [128, 1], F32)

        PS_S = [psum("pss0", [128, 2, 512], F32), psum("pss1", [128, 2, 512], F32)]
        PS_O = psum("pso", [128, 4, 512], F32)

        def sbank(i, b):
            return PS_S[i][:, b, :]

        def sbank16(i, b):
            return PS_S[i][:, b, :].bitcast(BF16)

        def obank(k):
            return PS_O[:, k, :]

        def KS(i, b):
            return ("S", i, b)

        def KO(k):
            return ("O", k)

        S.dma("sp", cs[:], cs_d, writes=["cs"])
        S.dma("sp", css[:], css_d, writes=["css"])
        S.dma("pool", ident[:], ident_d, writes=["ident"])
        S.dma("pool", epat[:], epat_d, writes=["epat"])
        S.op("dve", lambda e: e.memset(ones[:], 1.0), writes=["ones"])
        S.op("dve", lambda e: e.memset(onesf[:], 1.0), writes=["onesf"])
        for t in range(NT):
            S.dma("sp", h[:, t, :], x_d[t * 128:(t + 1) * 128, :], writes=[("h", t)])

        lam0 = 0.8 - 0.6 * math.exp(-0.3 * 0)
        S.dma("sp", lamb[:], alam_d.partition_broadcast(128), writes=["lamb"])
        S.op("dve", lambda e: e.tensor_tensor(lamt[:, 0:64], lamb[:, 0:64], lamb[:, 64:128], ALU.mult),
             reads=["lamb"], writes=["lamt0"])
        S.op("dve", lambda e: e.tensor_tensor(lamt[:, 64:128], lamb[:, 128:192], lamb[:, 192:256], ALU.mult),
             reads=["lamb"], writes=["lamt1"])
        S.op("dve", lambda e: e.reduce_sum(lams[:, 0:1], lamt[:, 0:64], AX.X), reads=["lamt0"], writes=["lams0"])
        S.op("dve", lambda e: e.reduce_sum(lams[:, 1:2], lamt[:, 64:128], AX.X), reads=["lamt1"], writes=["lams1"])
        S.op("act", lambda e: e.activation(out=lams[:, 2:4], in_=lams[:, 0:2], func=AF.Exp),
             reads=["lams0", "lams1"], writes=["lams2"])
        S.op("dve", lambda e: e.tensor_tensor(neglam[:], lams[:, 3:4], lams[:, 2:3], ALU.subtract),
             reads=["lams2"], writes=["neglam"])
        S.op("dve", lambda e: e.tensor_scalar(neglam[:], neglam[:], -lam0, 1.0, ALU.add, ALU.mult),
             reads=["neglam"], writes=["neglam"])
        S.dma("sp", gsub[:], asub_d.rearrange("(p o) -> p o", o=1), writes=["gsub"])
        S.op("dve", lambda e: e.tensor_scalar(gsub[:], gsub[:], 1.0 - lam0, 0.0, ALU.mult, ALU.add),
             reads=["gsub"], writes=["gsub"])

        def relbias_prep(stack):
            ext = sbuf(stack, "ext", [16, LEXT], F32)
            S.dma("sp", ext[:, 0:257], brel_d, writes=["ext"])
            S.op("dve", lambda e: e.tensor_copy(ext[:, 257:LEXT], ext[:, 256:257].broadcast_to([16, LEXT - 257])),
                 reads=["ext"], writes=["ext"])
            rep_v = rep_d.rearrange("(h r m) -> h r m", h=16, r=128)
            S.dma("sp", rep_v, ext[:].unsqueeze(1).broadcast_to([16, 128, LEXT]), reads=["ext"], writes=["rep"])

        def load_w(dst, src2d, key, nsplit=4):
            v = src2d.rearrange("(k p) n -> p k n", p=128)
            step = 8 // nsplit
            for i in range(nsplit):
                S.dma("pool", dst[:, i * step:(i + 1) * step, :], v[:, i * step:(i + 1) * step, :],
                      writes=[(key, i)])
            return [(key, i) for i in range(nsplit)]

        def stats_begin():
            S.op("dve", lambda e: e.memset(ss[:], 0.0), writes=["ss"])

        def stats_tile(t):
            S.op("act", lambda e, t=t, jk=junk: e.activation(out=jk[:], in_=h[:, t, :], func=AF.Square,
                                                             accum_out=ss[:, t:t + 1]),
                 reads=[("h", t), "ss"], writes=["junk", ("ss", t)])

        def stats_end():
            S.op("dve", lambda e: e.tensor_scalar(rstd[:], ss[:], 1.0 / D, EPS, ALU.mult, ALU.add),
                 reads=[("ss", t) for t in range(NT)], writes=["rstd"])
            S.op("act", lambda e: e.activation(out=rstd[:], in_=rstd[:], func=AF.Sqrt), reads=["rstd"], writes=["rstd"])
            S.op("dve", lambda e: e.reciprocal(rstd[:], rstd[:]), reads=["rstd"], writes=["rstd"])

        def norm_stats():
            stats_begin()
            for t in range(NT):
                stats_tile(t)
            stats_end()

        def norm_to_A(g_ap):
            S.dma("sp", grow[:], g_ap.partition_broadcast(128), writes=["grow"])
            for t in range(NT):
                xb = xn[t % 2]
                S.op("dve", lambda e, t=t, xb=xb: e.scalar_tensor_tensor(xb[:], h[:, t, :], rstd[:, t:t + 1], grow[:],
                                                                     ALU.mult, ALU.mult),
                     reads=[("h", t), "rstd", "grow"], writes=[("xn", t % 2)])
                pt = sbank16(t % 2, 0)

                def tr(e, xb=xb, pt=pt):
                    ins = None
                    for kc in range(8):
                        ins = e.transpose(pt[:, kc * 128:(kc + 1) * 128], xb[:, kc * 128:(kc + 1) * 128], ident[:])
                    return ins
                S.op("pe", tr, reads=[("xn", t % 2), "ident"], excl=[KS(t % 2, 0)])
                S.op("act", lambda e, t=t, pt=pt: e.activation(out=A[:, :, t * 128:(t + 1) * 128],
                                                               in_=pt.rearrange("p (k c) -> p k c", k=8), func=AF.Copy),
                     writes=[("A", t)], excl=[KS(t % 2, 0)])

        def Akeys(t0, t1):
            return [("A", t) for t in range(t0, t1)]

        def rope(src4, dst4, cosb, sinb, tc, ts, shape4, rkeys, wkeys, excl, tkey):
            S.op("dve", lambda e: e.tensor_tensor(tc, src4, cosb, ALU.mult), reads=rkeys, writes=[tkey + "c"], excl=excl)
            S.op("dve", lambda e: e.tensor_tensor(ts, src4, sinb, ALU.mult), reads=rkeys, writes=[tkey + "s"], excl=excl)
            S.op("pool", lambda e: e.tensor_tensor(dst4[:, :, 0, :], tc[:, :, 0, :], ts[:, :, 1, :], ALU.subtract),
                 reads=[tkey + "c", tkey + "s"], writes=wkeys)
            S.op("dve", lambda e: e.tensor_tensor(dst4[:, :, 1, :], tc[:, :, 1, :], ts[:, :, 0, :], ALU.add),
                 reads=[tkey + "c", tkey + "s"], writes=wkeys)

        def kv_slots():
            return [(PS_S[0][:, 0, :], KS(0, 0), PS_S[0][:, 1, :], KS(0, 1)),
                    (PS_S[1][:, 0, :], KS(1, 0), PS_S[1][:, 1, :], KS(1, 1)),
                    (obank(0), KO(0), obank(1), KO(1)),
                    (obank(2), KO(2), obank(3), KO(3))]

        def kv_phase(layer, u, Wkv, wkeys, out_k, out_v):
            slots = kv_slots()
            items = [("p", t) for t in range(NT - 1)] + [("s", s) for s in range(4)]

            def stage_a(idx):
                kind, t = items[idx]
                q = idx % 4
                bkv, kkv, btr, ktr = slots[q]
                P = 128 if kind == "p" else 32
                c0 = t * 128 if kind == "p" else SEQ + 32 * t
                pk = bkv[0:P, 0:256]

                def mm(e, c0=c0, P=P, pk=pk):
                    ins = None
                    for kc in range(8):
                        ins = e.matmul(pk, A[:, kc, c0:c0 + P], Wkv[:, kc, :], start=(kc == 0), stop=(kc == 7))
                    return ins
                S.op("pe", mm, reads=[("A", t if kind == "p" else 16)] + wkeys, excl=[kkv])
                kst = kstage[q % len(kstage)][0:P, :]
                qk = q % len(kstage)
                vst = vstage[q][0:P, :]
                k16q = k16[q % len(k16)][0:P, :]
                if layer == 0:
                    if kind == "p":
                        cosb = cs[:, t, 0:32].unsqueeze(1).unsqueeze(1).broadcast_to([128, 2, 2, 32])
                        sinb = cs[:, t, 32:64].unsqueeze(1).unsqueeze(1).broadcast_to([128, 2, 2, 32])
                        ck = "cs"
                    else:
                        cosb = css[:, 0:32].unsqueeze(1).unsqueeze(1).broadcast_to([32, 2, 2, 32])
                        sinb = css[:, 32:64].unsqueeze(1).unsqueeze(1).broadcast_to([32, 2, 2, 32])
                        ck = "css"
                    r4 = "p (g c f) -> p g c f"
                    rope(pk[:, 0:128].rearrange(r4, c=2, f=32), kst.rearrange(r4, c=2, f=32), cosb, sinb,
                         ktc[q][0:P, :].rearrange(r4, c=2, f=32), kts[q][0:P, :].rearrange(r4, c=2, f=32), None,
                         [ck], [("kst", qk)], [kkv], "kt%d" % q)
                else:
                    S.op("dve", lambda e, kst=kst, pk=pk: e.tensor_copy(kst, pk[:, 0:128]),
                         writes=[("kst", qk)], excl=[kkv])
                S.op("act", lambda e, vst=vst, pk=pk: e.activation(out=vst, in_=pk[:, 128:256], func=AF.Copy),
                     writes=[("vst", q)], excl=[kkv])
                S.op("act", lambda e, kst=kst, k16q=k16q: e.activation(out=k16q, in_=kst, func=AF.Copy),
                     reads=[("kst", qk)], writes=[("k16", q)])
                vt = V[:, t] if kind == "p" else Vs[:, t]
                if layer == 0:
                    S.op("pool", lambda e, vt=vt, vst=vst: e.tensor_copy(vt[:, 0, :], vst),
                         reads=[("vst", q)], writes=["V" if kind == "p" else "Vs"])
                else:
                    base = vt[:, 0, 0:1]
                    vdst = bass.AP(base.tensor, base.offset, [[base.ap[0][0], P], [192, 2], [1, 64]])
                    S.op("pool", lambda e, vdst=vdst, vst=vst: e.tensor_copy(vdst, vst.rearrange("p (a e) -> p a e", a=2)),
                         reads=[("vst", q)], writes=["V" if kind == "p" else "Vs"])
                if kind == "p":
                    need_out = (layer == 0) or (t >= 12)
                    r0 = t * 128 if layer == 0 else (t - 12) * 128
                else:
                    need_out = True
                    r0 = (SEQ if layer == 0 else 512) + 32 * t
                if need_out:
                    S.dma("sp", out_k[r0:r0 + P, u * 128:(u + 1) * 128], kst, reads=[("kst", qk)])
                    S.dma("sp", out_v[r0:r0 + P, u * 128:(u + 1) * 128], vst, reads=[("vst", q)])

            def stage_b(idx):
                kind, t = items[idx]
                q = idx % 4
                bkv, kkv, btr, ktr = slots[q]
                P = 128 if kind == "p" else 32
                c0 = t * 128 if kind == "p" else SEQ + 32 * t
                k16q = k16[q][0:P, :]
                ptk = btr.bitcast(BF16)
                S.op("pe", lambda e, ptk=ptk, k16q=k16q, P=P: e.transpose(ptk[:, 0:P], k16q, ident[0:P, 0:P]),
                     reads=[("k16", q), "ident"], excl=[ktr])
                S.op("dve", lambda e, c0=c0, P=P, ptk=ptk: e.tensor_copy(KT[:, c0:c0 + P], ptk[:, 0:P]),
                     writes=["KT"], excl=[ktr])

            SKEW = 3
            for idx in range(len(items) + SKEW):
                if idx < len(items):
                    stage_a(idx)
                if idx - SKEW >= 0:
                    stage_b(idx - SKEW)

        def finalize_head(layer, u, ncol, col0, bO, bL, tag, tb):
            outB = B[:, u, col0:col0 + ncol]
            tb = tb % len(T1)
            t1 = T1[tb][:, 0:ncol]
            k1 = ("T1", tb)
            if layer == 0:
                t2 = T2[tb][:, 0:ncol]
                k2 = ("T2", tb)
                o1, o2, l1, l2 = (obank(k)[:, 0:ncol] for k in (0, 1, 2, 3))
                t3 = T3[:, 0:ncol]
                t4 = T4[:, 0:ncol]
                S.op("dve", lambda e: e.tensor_copy(t3, o1), writes=["T3"], excl=[KO(0)])
                S.op("dve", lambda e: e.tensor_copy(t4, o2), writes=["T4"], excl=[KO(1)])
                S.op("act", lambda e: e.activation(out=t1, in_=l1, func=AF.Copy), writes=[k1], excl=[KO(2)])
                S.op("act", lambda e: e.activation(out=t2, in_=l2, func=AF.Copy), writes=[k2], excl=[KO(3)])
                S.op("dve", lambda e: e.tensor_tensor(t3, t3, t2, ALU.mult), reads=[k2, "T3"], writes=["T3"])
                S.op("dve", lambda e: e.tensor_tensor(t4, t4, t1, ALU.mult), reads=[k1, "T4"], writes=["T4"])
                S.op("dve", lambda e: e.scalar_tensor_tensor(t3, t4, neglam[:, 0:1], t3, ALU.mult, ALU.add),
                     reads=["T3", "T4", "neglam"], writes=["T3"])
                S.op("pool", lambda e: e.tensor_tensor(t4, t3, t3, ALU.mult), reads=["T3"], writes=["T4"])
                S.op("dve", lambda e: e.tensor_tensor(t1, t1, t2, ALU.mult), reads=[k1, k2], writes=[k1])
                S.op("dve", lambda e: e.scalar_tensor_tensor(t1, t1, EPS, t1, ALU.mult, ALU.mult), reads=[k1], writes=[k1])

                def tail(psq, kpsq):
                    S.op("pe", lambda e: e.matmul(psq[:, 0:ncol], onesf[:], t4, start=True, stop=True),
                         reads=["T4", "onesf"], excl=[kpsq])
                    S.op("dve", lambda e: e.scalar_tensor_tensor(t2, psq[:, 0:ncol], 1.0 / 128, t1, ALU.mult, ALU.add),
                         reads=[k1], writes=[k2], excl=[kpsq])
                    S.op("act", lambda e: e.activation(out=t2, in_=t2, func=AF.Ln), reads=[k2], writes=[k2])
                    S.op("act", lambda e: e.activation(out=t2, in_=t2, func=AF.Exp, scale=-0.5), reads=[k2], writes=[k2])
                    S.op("dve", lambda e: e.scalar_tensor_tensor(outB, t3, gsub[:, 0:1], t2, ALU.mult, ALU.mult),
                         reads=["T3", k2, "gsub"], writes=[("B", u, tag)])
                return tail
            o = obank(bO)[:, 0:ncol]
            l = obank(bL)[:, 0:ncol]
            t5 = T5[:, 0:ncol]
            S.op("dve", lambda e: e.tensor_copy(t5, o), writes=["T5"], excl=[KO(bO)])
            S.op("act", lambda e: e.activation(out=t1, in_=l, func=AF.Ln), writes=[k1], excl=[KO(bL)])
            S.op("act", lambda e: e.activation(out=t1, in_=t1, func=AF.Exp, scale=-1.0), reads=[k1], writes=[k1])
            S.op("dve", lambda e: e.tensor_tensor(outB, t5, t1, ALU.mult), reads=[k1, "T5"],
                 writes=[("B", u, tag)])
            return None

        def attention_prompt(layer, u):
            steps = []
            for j in range(4):
                if layer == 0:
                    kbs = list(range(0, 4 * j + 4))
                else:
                    kbs = list(range(max(0, 4 * j - 4), 4 * j + 4))
                    first = 4 * j - 2 if j >= 1 else 0
                    kbs.remove(first)
                    kbs.insert(0, first)
                for n, kb in enumerate(kbs):
                    st = dict(j=j, kb=kb, first=(n == 0), last=(n == len(kbs) - 1), diag=False, u0=0)
                    if layer == 0:
                        i = kb - 4 * j
                        st["c0"] = 128 * i if i >= 0 else 0
                        st["c1"] = 512
                        st["diag"] = i >= 0
                    else:
                        u0 = 512 * j - 128 * kb
                        st["u0"] = u0
                        st["c0"] = max(0, -u0)
                        st["c1"] = min(512, GW - u0)
                    steps.append(st)
            for n, st in enumerate(steps):
                st["n"] = n
            if layer == 0:
                sbufs = [(PS_S[0], [KS(0, 0), KS(0, 1)]), (PS_S[1], [KS(1, 0), KS(1, 1)])]
                look = 1
            else:
                sbufs = [(PS_S[0], [KS(0, 0), KS(0, 1)]), (PS_S[1], [KS(1, 0), KS(1, 1)]),
                         (PS_O[:, 2:4, :], [KO(2), KO(3)])]
                look = 2
            nsb = len(sbufs)
            npt = len(PT)

            def emit_qk(st):
                n, j, kb, c0, c1 = st["n"], st["j"], st["kb"], st["c0"], st["c1"]
                ps, pkeys = sbufs[n % nsb]
                pt = PT[n % npt]

                def mm(e):
                    ins = None
                    for a in range(2):
                        ins = e.matmul(ps[:, a, c0:c1], KT[64 * a:64 * a + 64, kb * 128:(kb + 1) * 128],
                                       B[64 * a:64 * a + 64, u, j * 512 + c0:j * 512 + c1], start=True, stop=True)
                    return ins
                S.op("pe", mm, reads=["KT", ("B", u, j)], excl=pkeys)
                if layer == 0:
                    S.op("act", lambda e: e.activation(out=pt[:, :, c0:c1], in_=ps[:, :, c0:c1], func=AF.Exp, scale=0.125),
                         writes=[("PT", n % npt)], excl=pkeys)
                    if st["diag"]:
                        S.op("dve", lambda e: e.memset(pt[64:128, :, c0:c0 + 64], 0.0), writes=[("PT", n % npt)])
                else:
                    u0 = st["u0"]
                    sb_ = SB[n % 2]
                    S.op("dve", lambda e: e.scalar_tensor_tensor(sb_[:, :, c0:c1], ps[:, :, c0:c1], 0.125,
                                                                 G[:, :, u0 + c0:u0 + c1], ALU.mult, ALU.add),
                         reads=["G"], writes=[("SB", n % 2)], excl=pkeys)
                    S.op("act", lambda e: e.activation(out=pt[:, :, c0:c1], in_=sb_[:, :, c0:c1], func=AF.Exp),
                         reads=[("SB", n % 2)], writes=[("PT", n % npt)])

            def emit_pv(st):
                n, j, kb, c0, c1 = st["n"], st["j"], st["kb"], st["c0"], st["c1"]
                pt = PT[n % npt]
                first, last = st["first"], st["last"]
                if layer == 0:
                    def mm(e):
                        e.matmul(obank(0)[:, c0:c1], V[:, kb, 0, :], pt[:, 0, c0:c1], start=first, stop=last)
                        e.matmul(obank(1)[:, c0:c1], V[:, kb, 0, :], pt[:, 1, c0:c1], start=first, stop=last)
                        e.matmul(obank(2)[:, c0:c1], ones[:], pt[:, 0, c0:c1], start=first, stop=last)
                        return e.matmul(obank(3)[:, c0:c1], ones[:], pt[:, 1, c0:c1], start=first, stop=last)
                    S.op("pe", mm, reads=[("PT", n % npt), "V", "ones"], excl=[KO(0), KO(1), KO(2), KO(3)])
                else:
                    def mm(e):
                        e.matmul(obank(0)[:, c0:c1], V[:, kb, 0, :], pt[:, 0, c0:c1], start=first, stop=False)
                        e.matmul(obank(0)[:, c0:c1], V[:, kb, 1, :], pt[:, 1, c0:c1], start=False, stop=last)
                        e.matmul(obank(1)[:, c0:c1], epat[:, 0, :], pt[:, 0, c0:c1], start=first, stop=False)
                        return e.matmul(obank(1)[:, c0:c1], epat[:, 1, :], pt[:, 1, c0:c1], start=False, stop=last)
                    S.op("pe", mm, reads=[("PT", n % npt), "V", "epat"], excl=[KO(0), KO(1)])

            for i in range(min(look, len(steps))):
                emit_qk(steps[i])
            pending = None
            for n, st in enumerate(steps):
                if n + look < len(steps):
                    emit_qk(steps[n + look])
                emit_pv(st)
                if pending is not None and (n >= pending[1] or n == len(steps) - 1):
                    ps, pkeys = sbufs[n % nsb]
                    pending[0](ps[:, 0, :], pkeys[0])
                    pending = None
                if st["last"]:
                    j = st["j"]
                    tail = finalize_head(layer, u, 512, j * 512, 0, 1, j, j % 2)
                    if tail is not None:
                        if n == len(steps) - 1:
                            return tail
                        pending = (tail, n + 3)
            return None

        def sample_load(layer, u, s):
            nblk = 8 if layer == 0 else 4
            ck_d, cv_d = (cak_d, cav_d) if layer == 0 else (cbk_d, cbv_d)
            w = s % 2
            kv = ck_d[s, :, u * 128:(u + 1) * 128].rearrange("(b p) d -> p b d", p=128)
            S.dma("pool", CK[w][:, 0:nblk, :], kv, writes=[("CK", w)])
            if layer == 0:
                vv = cv_d[s, :, u * 128:(u + 1) * 128].rearrange("(b p) d -> p b d", p=128)
                S.dma("pool", CV[w][:, 0:nblk, 0, :], vv, writes=[("CV", w)])
            else:
                for a in range(2):
                    vv = cv_d[s, :, u * 128 + 64 * a:u * 128 + 64 * a + 64].rearrange("(b p) d -> p b d", p=128)
                    S.dma("pool", CV[w][:, 0:nblk, a, 64 * a:64 * a + 64], vv, writes=[("CV", w)])

        def attention_sample(layer, u, last_tail):
            nblk = 8 if layer == 0 else 4
            ncs = nblk * 32

            def st1(s):
                w = s % 2
                pb = s % 2
                ck, ckt = CK[w], CKT[w]
                ptk = sbank16(pb, 0)

                def tr(e):
                    ins = None
                    for b in range(nblk):
                        ins = e.transpose(ptk[:, b * 128:(b + 1) * 128], ck[:, b, :], ident[:])
                    return ins
                S.op("pe", tr, reads=[("CK", w), "ident"], excl=[KS(pb, 0)])
                S.op("act", lambda e: e.activation(out=ckt[:, 0:nblk * 128].rearrange("p (k c) -> p k c", c=128),
                                                   in_=ptk[:, 0:nblk * 128].rearrange("p (k c) -> p k c", c=128), func=AF.Copy),
                     writes=[("CKT", w)], excl=[KS(pb, 0)])

            def st2(s):
                w = s % 2
                pb = s % 2
                c0 = SEQ + 32 * s
                ckt = CKT[w]
                bS = [sbank(pb, 1), sbank(pb, 0)]

                def mmS(e):
                    ins = None
                    for a in range(2):
                        for b in range(nblk):
                            slot = b if layer == 0 else nblk - 1 - b
                            ins = e.matmul(bS[a][:, slot * 32:slot * 32 + 32],
                                           ckt[64 * a:64 * a + 64, b * 128:(b + 1) * 128],
                                           B[64 * a:64 * a + 64, u, c0:c0 + 32], start=True, stop=True)
                        ins = e.matmul(bS[a][0:32, 256:288], KT[64 * a:64 * a + 64, c0:c0 + 32],
                                       B[64 * a:64 * a + 64, u, c0:c0 + 32], start=True, stop=True)
                    return ins
                S.op("pe", mmS, reads=[("CKT", w), "KT", ("B", u, 4)], excl=[KS(pb, 0), KS(pb, 1)])
                pts, ptn = PTs[w], PTn[w]
                for a in range(2):
                    ka = KS(pb, 1 - a)
                    if layer == 0:
                        S.op("act", lambda e, a=a: e.activation(out=pts[:, a, 0:ncs], in_=bS[a][:, 0:ncs], func=AF.Exp, scale=0.125),
                             writes=[("PTs", w)], excl=[ka])
                        S.op("act", lambda e, a=a: e.activation(out=ptn[:, a, :], in_=bS[a][0:32, 256:288], func=AF.Exp, scale=0.125),
                             writes=[("PTn", w)], excl=[ka])
                    else:
                        sb_ = SB[w]
                        gs = G[:, a, 128:640].rearrange("p (s x) -> p s x", x=128)[:, :, 0:32]
                        S.op("dve", lambda e, a=a, gs=gs, sb_=sb_: e.scalar_tensor_tensor(
                            sb_[:, a, 0:ncs].rearrange("p (s x) -> p s x", x=32),
                            bS[a][:, 0:ncs].rearrange("p (s x) -> p s x", x=32), 0.125, gs, ALU.mult, ALU.add),
                            reads=["G"], writes=[("SB", w)], excl=[ka])
                        S.op("dve", lambda e, a=a, sb_=sb_: e.scalar_tensor_tensor(
                            sb_[0:32, a, 256:288], bS[a][0:32, 256:288], 0.125, G[0:32, a, 0:32], ALU.mult, ALU.add),
                            reads=["G"], writes=[("SB", w)], excl=[ka])
                        S.op("act", lambda e, a=a, sb_=sb_: e.activation(out=pts[:, a, 0:ncs], in_=sb_[:, a, 0:ncs], func=AF.Exp),
                             reads=[("SB", w)], writes=[("PTs", w)])
                        S.op("act", lambda e, a=a, sb_=sb_: e.activation(out=ptn[:, a, :], in_=sb_[0:32, a, 256:288], func=AF.Exp),
                             reads=[("SB", w)], writes=[("PTn", w)])

            def st3(s):
                w = s % 2
                cv, pts, ptn = CV[w], PTs[w], PTn[w]

                def mmPV(e):
                    ins = None
                    cs0 = s * 32
                    for b in range(nblk):
                        slot = b if layer == 0 else nblk - 1 - b
                        for a in range(2):
                            rhs = pts[:, a, slot * 32:slot * 32 + 32]
                            if layer == 0:
                                e.matmul(obank(a)[:, cs0:cs0 + 32], cv[:, b, 0, :], rhs, start=(b == 0), stop=False)
                                ins = e.matmul(obank(2 + a)[:, cs0:cs0 + 32], ones[:], rhs, start=(b == 0), stop=False)
                            else:
                                st0 = (b == 0 and a == 0)
                                e.matmul(obank(0)[:, cs0:cs0 + 32], cv[:, b, a, :], rhs, start=st0, stop=False)
                                ins = e.matmul(obank(1)[:, cs0:cs0 + 32], epat[:, a, :], rhs, start=st0, stop=False)
                    for a in range(2):
                        rhs = ptn[0:32, a, :]
                        if layer == 0:
                            e.matmul(obank(a)[:, cs0:cs0 + 32], Vs[0:32, s, 0, :], rhs, start=False, stop=True)
                            ins = e.matmul(obank(2 + a)[:, cs0:cs0 + 32], ones[0:32, :], rhs, start=False, stop=True)
                        else:
                            e.matmul(obank(0)[:, cs0:cs0 + 32], Vs[0:32, s, a, :], rhs, start=False, stop=(a == 1))
                            ins = e.matmul(obank(1)[:, cs0:cs0 + 32], epat[0:32, a, :], rhs, start=False, stop=(a == 1))
                    return ins
                ex = [KO(0), KO(1), KO(2), KO(3)] if layer == 0 else [KO(0), KO(1)]
                S.op("pe", mmPV, reads=[("PTs", w), ("PTn", w), ("CV", w), "Vs", "ones", "epat"], excl=ex)
                if s + 2 < 4:
                    sample_load(layer, u, s + 2)

            st1(0)
            st1(1)
            st2(0)
            if last_tail is not None:
                last_tail(sbank(1, 1), KS(1, 1))
            st2(1); st3(0); st1(2); st2(2); st3(1); st1(3); st2(3); st3(2); st3(3)
            tail = finalize_head(layer, u, 128, SEQ, 0, 1, 4, 0)
            if tail is not None:
                tail(sbank(0, 0), KS(0, 0))

        def out_proj(W, wkeys):
            stats_begin()
            for t in range(NT):
                pb = t % 2
                ps = PS_S[pb]
                jt = min(t // 4, 4)

                def mm(e, t=t, ps=ps):
                    ins = None
                    for nh in range(2):
                        for kc in range(8):
                            ins = e.matmul(ps[:, nh, :], B[:, kc, t * 128:(t + 1) * 128], W[:, kc, nh * 512:(nh + 1) * 512],
                                           start=(kc == 0), stop=(kc == 7))
                    return ins
                S.op("pe", mm, reads=[("B", kc, jt) for kc in range(8)] + wkeys, excl=[KS(pb, 0), KS(pb, 1)])
                S.op("dve", lambda e, t=t, ps=ps: e.tensor_tensor(h[:, t, :], ps[:].rearrange("p a n -> p (a n)"), h[:, t, :], ALU.add),
                     reads=[("h", t)], writes=[("h", t)], excl=[KS(pb, 0), KS(pb, 1)])
                stats_tile(t)
            stats_end()

        def mlp(l):
            norm_to_A(g_mlp_d[l])
            if l == 0:
                relbias_prep(es_m)
            stats_begin()
            blocks = [(0, 512), (512, 512), (1024, 512), (1536, 512), (2048, 128)]
            wkeys = {}

            def load_group(fg):
                k1 = load_w(W1b[fg % 2], wff1_d[l][:, fg * 1024:(fg + 1) * 1024], ("W1", fg % 2))
                k2 = load_w(W2b[fg % 2], wff2_d[l][fg * 1024:(fg + 1) * 1024, :], ("W2", fg % 2))
                wkeys[fg] = (k1, k2)

            def up(fg, bi, seq):
                W1 = W1b[fg % 2]
                k1 = wkeys[fg][0]
                c0, n = blocks[bi]
                Hb = H1[seq % 2]
                for fc in range(8):
                    hb = fc % 4
                    ph = obank(hb)[:, 0:n]

                    def mm(e, fc=fc, ph=ph):
                        ins = None
                        for kc in range(8):
                            ins = e.matmul(ph, W1[:, kc, fc * 128:(fc + 1) * 128], A[:, kc, c0:c0 + n],
                                           start=(kc == 0), stop=(kc == 7))
                        return ins
                    S.op("pe", mm, reads=Akeys(c0 // 128, (c0 + n) // 128) + k1, excl=[KO(hb)])
                    rl = RL[fc % 2]
                    S.op("act", lambda e, ph=ph, rl=rl: e.activation(out=rl[:, 0:n], in_=ph, func=AF.Relu),
                         writes=[("RL", fc % 2)], excl=[KO(hb)])
                    S.op("pool", lambda e, fc=fc, rl=rl: e.tensor_tensor(Hb[:, fc, 0:n], rl[:, 0:n], rl[:, 0:n], ALU.mult),
                         reads=[("RL", fc % 2)], writes=[("H1", seq % 2, fc)])

            def down(fg, bi, seq):
                W2 = W2b[fg % 2]
                k2 = wkeys[fg][1]
                c0, n = blocks[bi]
                Hb = H1[seq % 2]
                for tt in range(n // 128):
                    t = c0 // 128 + tt
                    pb = t % 2
                    ps = PS_S[pb]

                    def mm2(e, tt=tt, ps=ps):
                        ins = None
                        for nh in range(2):
                            for fc in range(8):
                                ins = e.matmul(ps[:, nh, :], Hb[:, fc, tt * 128:(tt + 1) * 128],
                                               W2[:, fc, nh * 512:(nh + 1) * 512], start=(fc == 0), stop=(fc == 7))
                        return ins
                    S.op("pe", mm2, reads=[("H1", seq % 2, fc) for fc in range(8)] + k2, excl=[KS(pb, 0), KS(pb, 1)])
                    S.op("dve", lambda e, t=t, ps=ps: e.tensor_tensor(h[:, t, :], ps[:].rearrange("p a n -> p (a n)"),
                                                                  h[:, t, :], ALU.add),
                         reads=[("h", t)], writes=[("h", t)], excl=[KS(pb, 0), KS(pb, 1)])
                    if fg == 3:
                        stats_tile(t)

            work = [(fg, bi) for fg in range(4) for bi in range(len(blocks))]
            load_group(0)
            for i, (fg, bi) in enumerate(work):
                if bi == 1 and fg + 1 < 4:
                    load_group(fg + 1)
                if i == 0:
                    up(fg, bi, i)
                if i + 1 < len(work):
                    up(work[i + 1][0], work[i + 1][1], i + 1)
                down(fg, bi, i)
            stats_end()

        for layer in range(2):
            with ExitStack() as es_l:
                B = sbuf(es_l, "B%d" % layer, [128, 8, NTOK], BF16)
                with ExitStack() as es_q:
                    grow = sbuf(es_q, "grow", [128, D], F32)
                    xn = [sbuf(es_q, "xn0", [128, D], BF16), sbuf(es_q, "xn1", [128, D], BF16)]
                    junk = sbuf(es_q, "junk", [128, D], BF16)
                    Wq = sbuf(es_q, "Wq", [128, 8, D], BF16)
                    if layer == 0:
                        norm_stats()
                        tcq = [sbuf(es_q, "tcq%d" % i, [128, D], F32) for i in range(2)]
                        tsq = [sbuf(es_q, "tsq%d" % i, [128, D], F32) for i in range(2)]
                        qr = [sbuf(es_q, "qr%d" % i, [128, D], BF16) for i in range(2)]
                        wk = load_w(Wq, wqkv_d[:, 0:D], "Wq")
                        norm_to_A(g_attn_d[0])
                        r4 = "p (g c f) -> p g c f"

                        def q_a(t):
                            pb = t % 2
                            ps = PS_S[pb]

                            def mm(e):
                                ins = None
                                for nh in range(2):
                                    for kc in range(8):
                                        ins = e.matmul(ps[:, nh, :], A[:, kc, t * 128:(t + 1) * 128],
                                                       Wq[:, kc, nh * 512:(nh + 1) * 512], start=(kc == 0), stop=(kc == 7))
                                return ins
                            S.op("pe", mm, reads=[("A", t)] + wk, excl=[KS(pb, 0), KS(pb, 1)])
                            cosb = cs[:, t, 0:32].unsqueeze(1).unsqueeze(1).broadcast_to([128, 16, 2, 32])
                            sinb = cs[:, t, 32:64].unsqueeze(1).unsqueeze(1).broadcast_to([128, 16, 2, 32])
                            src4 = ps[:].rearrange("p a (g c f) -> p (a g) c f", c=2, f=32)
                            tc = tcq[pb][:].rearrange(r4, c=2, f=32)
                            ts = tsq[pb][:].rearrange(r4, c=2, f=32)
                            dst4 = qr[pb][:].rearrange(r4, c=2, f=32)
                            ex = [KS(pb, 0), KS(pb, 1)]
                            S.op("dve", lambda e: e.tensor_tensor(tc, src4, cosb, ALU.mult), reads=["cs"], writes=[("tqc", pb)], excl=ex)
                            S.op("dve", lambda e: e.tensor_tensor(ts, src4, sinb, ALU.mult), reads=["cs"], writes=[("tqs", pb)], excl=ex)
                            S.op("pool", lambda e: e.tensor_tensor(dst4[:, :, 0, :], tc[:, :, 0, :], ts[:, :, 1, :], ALU.subtract),
                                 reads=[("tqc", pb), ("tqs", pb)], writes=[("qr0", pb)])
                            S.op("dve", lambda e: e.tensor_tensor(dst4[:, :, 1, :], tc[:, :, 1, :], ts[:, :, 0, :], ALU.add),
                                 reads=[("tqc", pb), ("tqs", pb)], writes=[("qr1", pb)])

                        def q_b(t):
                            pb = t % 2
                            pt = obank(t % 4).bitcast(BF16)
                            qrt = qr[pb]

                            def tr(e):
                                ins = None
                                for kc in range(8):
                                    ins = e.transpose(pt[:, kc * 128:(kc + 1) * 128], qrt[:, kc * 128:(kc + 1) * 128], ident[:])
                                return ins
                            S.op("pe", tr, reads=[("qr0", pb), ("qr1", pb), "ident"], excl=[KO(t % 4)])
                            jt = min(t // 4, 4)
                            S.op("act", lambda e: e.activation(out=B[:, :, t * 128:(t + 1) * 128],
                                                               in_=pt.rearrange("p (k c) -> p k c", k=8), func=AF.Copy),
                                 writes=[("B", kc, jt) for kc in range(8)], excl=[KO(t % 4)])
                        for t in range(NT + 1):
                            if t < NT:
                                q_a(t)
                            if t >= 1:
                                q_b(t - 1)
                    else:
                        wk = load_w(Wq, wbq_d, "Wq")
                        norm_to_A(g_attn_d[1])
                        blocks = [(0, 512), (512, 512), (1024, 512), (1536, 512), (2048, 128)]
                        n_ = 0
                        for c in range(8):
                            for bi, (c0, n) in enumerate(blocks):
                                pk = obank(n_ % 4)[:, 0:n]

                                def mm(e, c=c, c0=c0, n=n, pk=pk):
                                    ins = None
                                    for kc in range(8):
                                        ins = e.matmul(pk, Wq[:, kc, c * 128:(c + 1) * 128], A[:, kc, c0:c0 + n],
                                                       start=(kc == 0), stop=(kc == 7))
                                    return ins
                                S.op("pe", mm, reads=Akeys(c0 // 128, (c0 + n) // 128) + wk, excl=[KO(n_ % 4)])
                                S.op("act", lambda e, c=c, c0=c0, n=n, pk=pk: e.activation(out=B[:, c, c0:c0 + n], in_=pk, func=AF.Copy),
                                     writes=[("B", c, bi)], excl=[KO(n_ % 4)])
                                n_ += 1
                        norm_to_A(g_kv_d)
                    S.end_phase()
                with ExitStack() as es_a:
                    KT = sbuf(es_a, "KT", [128, NTOK], BF16)
                    V = sbuf(es_a, "V", [128, 16, 2, 128], BF16)
                    Vs = sbuf(es_a, "Vs", [32, 4, 2, 128], BF16)
                    nblk_l = 8 if layer == 0 else 4
                    npt_l = 2 if layer == 0 else 3
                    PT = [sbuf(es_a, "PT%d" % i, [128, 2, 512], BF16) for i in range(npt_l)]
                    T1 = [sbuf(es_a, "T1a", [128, 512], F32)]
                    if layer == 0:
                        T1.append(sbuf(es_a, "T1b", [128, 512], F32))
                    if layer == 0:
                        T2 = [sbuf(es_a, "T2a", [128, 512], F32), sbuf(es_a, "T2b", [128, 512], F32)]
                        T3 = sbuf(es_a, "T3", [128, 512], F32)
                        T4 = sbuf(es_a, "T4", [128, 512], F32)
                    CK = [sbuf(es_a, "CK%d" % i, [128, nblk_l, 128], BF16) for i in range(2)]
                    CV = [sbuf(es_a, "CV%d" % i, [128, nblk_l, 2 if layer == 1 else 1, 128], BF16) for i in range(2)]
                    CKT = [sbuf(es_a, "CKT%d" % i, [128, nblk_l * 128], BF16) for i in range(2)]
                    PTs = [sbuf(es_a, "PTs%d" % i, [128, 2, 256], BF16) for i in range(2)]
                    PTn = [sbuf(es_a, "PTn%d" % i, [32, 2, 32], BF16) for i in range(2)]
                    Wkvb = [sbuf(es_a, "Wkv0", [128, 8, 256], BF16), sbuf(es_a, "Wkv1", [128, 8, 256], BF16)]
                    kstage = [sbuf(es_a, "kst%d" % i, [128, 128], F32) for i in range(4 if layer == 0 else 2)]
                    vstage = [sbuf(es_a, "vst%d" % i, [128, 128], F32) for i in range(4)]
                    k16 = [sbuf(es_a, "k16_%d" % i, [128, 128], BF16) for i in range(4)]
                    if layer == 0:
                        ktc = [sbuf(es_a, "ktc%d" % i, [128, 128], F32) for i in range(4)]
                        kts = [sbuf(es_a, "kts%d" % i, [128, 128], F32) for i in range(4)]
                    if layer == 1:
                        T5 = sbuf(es_a, "T5", [128, 512], F32)
                        SB = [sbuf(es_a, "SB%d" % i, [128, 2, 512], F32) for i in range(2)]
                        G = sbuf(es_a, "G", [128, 2, GW], F32)
                        Mk = sbuf(es_a, "Mk", [128, GW], F32)
                        S.dma("sp", Mk[:], mask_d, writes=["Mk"])
                        S.op("pool", lambda e: e.memset(V[:], 0.0), writes=["V"])
                        S.op("pool", lambda e: e.memset(Vs[:], 0.0), writes=["Vs"])
                        for i in range(2):
                            S.op("pool", lambda e, i=i: e.memset(CV[i][:], 0.0), writes=[("CV", i)])
                    out_k, out_v = (ak_d, av_d) if layer == 0 else (bk_d, bv_d)
                    def load_wkv(u):
                        Wkv = Wkvb[u % 2]
                        if layer == 0:
                            srcs = [wqkv_d[:, D + u * 128:D + (u + 1) * 128], wqkv_d[:, 2 * D + u * 128:2 * D + (u + 1) * 128]]
                        else:
                            srcs = [wkv_d[:, u * 128:(u + 1) * 128], wkv_d[:, D + u * 128:D + (u + 1) * 128]]
                        for i, src in enumerate(srcs):
                            S.dma("pool", Wkv[:, :, i * 128:(i + 1) * 128], src.rearrange("(k p) n -> p k n", p=128),
                                  writes=[("Wkv", u % 2, i)])
                    load_wkv(0)
                    for u in range(8):
                        Wkv = Wkvb[u % 2]
                        wkeys = [("Wkv", u % 2, 0), ("Wkv", u % 2, 1)]
                        if layer == 1:
                            for a in range(2):
                                src = bass.AP(rep_d.tensor, (2 * u + a) * 128 * LEXT + 128, [[LEXT - 1, 128], [1, GW]])
                                S.dma("sp", G[:, a, :], src, reads=["rep"], writes=["G"])
                            S.op("pool", lambda e: e.tensor_tensor(G[:], G[:], Mk[:].unsqueeze(1).broadcast_to([128, 2, GW]), ALU.add),
                                 reads=["G", "Mk"], writes=["G"])
                        kv_phase(layer, u, Wkv, wkeys, out_k, out_v)
                        if u + 1 < 8:
                            load_wkv(u + 1)
                        sample_load(layer, u, 0)
                        sample_load(layer, u, 1)
                        last_tail = attention_prompt(layer, u)
                        attention_sample(layer, u, last_tail)
                    S.end_phase()
                with ExitStack() as es_o:
                    Wo = sbuf(es_o, "Wo", [128, 8, D], BF16)
                    junk = sbuf(es_o, "junko", [128, D], BF16)
                    wk = load_w(Wo, wao_d if layer == 0 else wbo_d, "Wo")
                    out_proj(Wo, wk)
                    S.end_phase()
            with ExitStack() as es_m:
                grow = sbuf(es_m, "growm", [128, D], F32)
                xn = [sbuf(es_m, "xnm0", [128, D], BF16), sbuf(es_m, "xnm1", [128, D], BF16)]
                junk = sbuf(es_m, "junkm", [128, D], BF16)
                W1b = [sbuf(es_m, "W1a", [128, 8, D], BF16), sbuf(es_m, "W1b", [128, 8, D], BF16)]
                W2b = [sbuf(es_m, "W2a", [128, 8, D], BF16), sbuf(es_m, "W2b", [128, 8, D], BF16)]
                H1 = [sbuf(es_m, "H1a", [128, 8, 512], BF16), sbuf(es_m, "H1b", [128, 8, 512], BF16)]
                RL = [sbuf(es_m, "RL0", [128, 512], F32), sbuf(es_m, "RL1", [128, 512], F32)]
                mlp(layer)
                S.end_phase()
        with ExitStack() as es_f:
            grow = sbuf(es_f, "growf", [128, D], F32)
            junk = sbuf(es_f, "junkf", [128, D], BF16)
            ys = [sbuf(es_f, "ys%d" % i, [128, D], F32) for i in range(3)]
            S.dma("sp", grow[:], g_fin_d.partition_broadcast(128), writes=["grow"])
            for t in range(NT):
                yb = ys[t % 3]
                S.op("dve",
                     lambda e, t=t, yb=yb: e.scalar_tensor_tensor(yb[:], h[:, t, :], rstd[:, t:t + 1], grow[:],
                                                                  ALU.mult, ALU.mult),
                     reads=[("h", t), "rstd", "grow"], writes=[("ys", t % 3)])
                S.dma("sp", y_d[t * 128:(t + 1) * 128, :], yb[:], reads=[("ys", t % 3)])
            S.end_phase()
    return nc


def _const_tables():
    half = 32
    inv = 1.0 / (10000.0 ** (np.arange(half, dtype=np.float32) / half))
    pos = np.zeros((128, NT), np.float32)
    for t in range(16):
        pos[:, t] = t * 128 + np.arange(128)
    pos[:, 16] = PAST + (np.arange(128) % 32)
    ang = pos[:, :, None] * inv[None, None, :]
    cs = np.concatenate([np.cos(ang), np.sin(ang)], axis=-1).astype(np.float32)
    angs = (PAST + np.arange(32, dtype=np.float32))[:, None] * inv[None, :]
    css = np.concatenate([np.cos(angs), np.sin(angs)], axis=-1).astype(np.float32)
    ident = np.eye(128, dtype=np.float32)
    kl = np.arange(128)[:, None] // 64
    uu = np.arange(GW)[None, :] // 64
    dlt = uu - kl
    mask = np.where((dlt >= 0) & (dlt <= 8), 0.0, NEG).astype(np.float32)
    epat = np.zeros((128, 2, 128), np.float32)
    epat[:, 0, 0:64] = 1.0
    epat[:, 1, 64:128] = 1.0
    return cs, css, ident, mask, epat


_NC_CACHE = {}


def kernel(x_prompt, x_sample, cache_a_k, cache_a_v, cache_b_k, cache_b_v,
           g_attn, w_a_qkv, a_lambda, a_subln, w_a_o, g_kv, w_kv, w_b_q, b_rel, w_b_o,
           g_mlp, w_ff1, w_ff2, g_final):
    f = lambda a: np.ascontiguousarray(np.asarray(a, dtype=np.float32))
    x_prompt, x_sample = f(x_prompt), f(x_sample)
    cache_a_k, cache_a_v, cache_b_k, cache_b_v = f(cache_a_k), f(cache_a_v), f(cache_b_k), f(cache_b_v)
    cs, css, ident, mask, epat = _const_tables()
    shared = {
        "g_attn": f(g_attn), "w_a_qkv": f(w_a_qkv)[0], "a_lambda": f(a_lambda).reshape(256),
        "a_subln": f(a_subln).reshape(128), "w_a_o": f(w_a_o)[0], "g_kv": f(g_kv), "w_kv": f(w_kv),
        "w_b_q": f(w_b_q)[0], "b_rel": f(b_rel)[0], "w_b_o": f(w_b_o)[0], "g_mlp": f(g_mlp),
        "w_ff1": f(w_ff1), "w_ff2": f(w_ff2), "g_final": f(g_final),
        "c_cs": cs, "c_css": css, "c_ident": ident, "c_mask": mask, "c_epat": epat,
    }
    in_maps = []
    for c in range(N_CORES):
        m = dict(shared)
        m["x"] = np.concatenate([x_prompt[c], x_sample[4 * c:4 * c + 4].reshape(128, D)], axis=0)
        m["cak"] = cache_a_k[0, 4 * c:4 * c + 4].reshape(4, PAST, D)
        m["cav"] = cache_a_v[0, 4 * c:4 * c + 4].reshape(4, PAST, D)
        m["cbk"] = cache_b_k[4 * c:4 * c + 4].reshape(4, 512, D)
        m["cbv"] = cache_b_v[4 * c:4 * c + 4].reshape(4, 512, D)
        in_maps.append(m)
    if "nc" not in _NC_CACHE:
        _NC_CACHE["nc"] = build_nc()
    res = run_bass_kernel_spmd(_NC_CACHE["nc"], in_maps, core_ids=list(range(N_CORES)))
    R = res.results
    y_p = np.stack([R[c]["y"][:SEQ] for c in range(N_CORES)])
    y_s = np.concatenate([R[c]["y"][SEQ:].reshape(4, 32, D) for c in range(N_CORES)])
    ak_p = np.stack([R[c]["ak"][:SEQ].reshape(SEQ, 8, 128) for c in range(N_CORES)])[None]
    av_p = np.stack([R[c]["av"][:SEQ].reshape(SEQ, 8, 128) for c in range(N_CORES)])[None]
    ak_s = np.concatenate([R[c]["ak"][SEQ:].reshape(4, 32, 8, 128) for c in range(N_CORES)])[None]
    av_s = np.concatenate([R[c]["av"][SEQ:].reshape(4, 32, 8, 128) for c in range(N_CORES)])[None]
    bk_p = np.stack([R[c]["bk"][:512].reshape(512, 16, 64) for c in range(N_CORES)])
    bv_p = np.stack([R[c]["bv"][:512].reshape(512, 16, 64) for c in range(N_CORES)])
    bk_s = np.concatenate([R[c]["bk"][512:].reshape(4, 32, 16, 64) for c in range(N_CORES)])
    bv_s = np.concatenate([R[c]["bv"][512:].reshape(4, 32, 16, 64) for c in range(N_CORES)])
    outs = (y_p, y_s, ak_p, av_p, bk_p, bv_p, ak_s, av_s, bk_s, bv_s)
    return tuple(np.ascontiguousarray(o, dtype=np.float32) for o in outs)
```

```python
import math
from contextlib import ExitStack

import numpy as np

import concourse.bass as bass
import concourse.mybir as mybir
from concourse.bass_utils import run_bass_kernel_spmd

F32 = mybir.dt.float32
BF16 = mybir.dt.bfloat16
AF = mybir.ActivationFunctionType
ALU = mybir.AluOpType
AX = mybir.AxisListType

D = 1024
SEQ = 2048
NT = 17
NTOK = NT * 128
PAST = 1024
DFF = 4096
EPS = 1e-6
LEXT = 896
GW = 768
NEG = -30000.0
N_CORES = 8


class Sched:
    ENG = ("pe", "act", "dve", "pool", "sp")

    def __init__(self, nc, es, ndma=24):
        self.nc = nc
        self.ops = {e: [] for e in self.ENG}
        self.sem = {e: es.enter_context(nc.semaphore("s_" + e)) for e in self.ENG}
        self.cnt = {e: 0 for e in self.ENG}
        self.dsem = [es.enter_context(nc.semaphore("d%d" % i)) for i in range(ndma)]
        self.dcnt = [0] * ndma
        self.dnext2 = [0, 0]
        self.waited = {e: {} for e in self.ENG}
        self.lastw = {}
        self.readers = {}
        self.excl = {}

    def _deps(self, eng, reads, writes, excl):
        deps = {}

        def add(k, v):
            if k == "pe" and eng == "pe":
                return
            if deps.get(k, 0) < v:
                deps[k] = v
        for b in reads:
            ev = self.lastw.get(b)
            if ev is not None:
                add(*ev)
        for b in writes:
            ev = self.lastw.get(b)
            if ev is not None:
                add(*ev)
            for k, v in self.readers.get(b, {}).items():
                add(k, v)
        for b in excl:
            for k, v in self.excl.get(b, {}).items():
                if k != eng:
                    add(k, v)
        out = []
        w = self.waited[eng]
        for k, v in deps.items():
            if w.get(k, 0) < v:
                w[k] = v
                out.append((k, v))
        return out

    def _semof(self, k):
        return self.sem[k] if isinstance(k, str) else self.dsem[k]

    def _mark(self, ev, reads, writes, excl):
        k, v = ev
        for b in reads:
            r = self.readers.setdefault(b, {})
            if r.get(k, 0) < v:
                r[k] = v
        for b in writes:
            self.lastw[b] = ev
            self.readers[b] = {}
        for b in excl:
            self.excl[b] = {k: v}

    def op(self, eng, fn, reads=(), writes=(), excl=()):
        waits = self._deps(eng, reads, writes, excl)
        self.cnt[eng] += 1
        ev = (eng, self.cnt[eng])
        sem = self.sem[eng]
        waitl = [(self._semof(k), v) for k, v in waits]

        def emit(e):
            for s, v in waitl:
                e.wait_ge(s, v)
            fn(e).then_inc(sem, 1)
        self.ops[eng].append(emit)
        self._mark(ev, reads, writes, excl)

    def dma(self, eng, out, in_, reads=(), writes=()):
        half = len(self.dsem) // 2
        qi = 0 if eng == "pool" else 1
        k = qi * half + self.dnext2[qi]
        self.dnext2[qi] = (self.dnext2[qi] + 1) % half
        waits = self._deps(eng, reads, writes, ())
        prev = self.dcnt[k]
        if prev and self.waited[eng].get(k, 0) < prev:
            self.waited[eng][k] = prev
            waits.append((k, prev))
        self.dcnt[k] += 16
        ev = (k, self.dcnt[k])
        sem = self.dsem[k]
        waitl = [(self._semof(kk), v) for kk, v in waits]

        def emit(e):
            for s, v in waitl:
                e.wait_ge(s, v)
            e.dma_start(out=out, in_=in_).then_inc(sem, 16)
        self.ops[eng].append(emit)
        self._mark(ev, reads, writes, ())

    def barrier(self):
        waitl = []
        for e in self.ENG:
            if self.cnt[e]:
                waitl.append((e, self.cnt[e]))
        for k in range(len(self.dsem)):
            if self.dcnt[k]:
                waitl.append((k, self.dcnt[k]))
        for eng in self.ENG:
            mine = []
            for k, v in waitl:
                if self.waited[eng].get(k, 0) < v:
                    self.waited[eng][k] = v
                    mine.append((self._semof(k), v))

            def emit(e, mine=mine):
                for s, v in mine:
                    e.wait_ge(s, v)
            self.ops[eng].append(emit)
        self.lastw.clear()
        self.readers.clear()
        self.excl.clear()

    def end_phase(self):
        self.barrier()
        self.replay()

    def replay(self):
        ops = self.ops
        self.ops = {e: [] for e in self.ENG}
        with self.nc.Block() as block:
            @block.tensor
            def _(e):
                for f in ops["pe"]:
                    f(e)

            @block.scalar
            def _(e):
                for f in ops["act"]:
                    f(e)

            @block.vector
            def _(e):
                for f in ops["dve"]:
                    f(e)

            @block.gpsimd
            def _(e):
                for f in ops["pool"]:
                    f(e)

            @block.sync
            def _(e):
                for f in ops["sp"]:
                    f(e)


def build_nc():
    nc = bass.Bass("TRN2", target_bir_lowering=False)

    def din(name, shape):
        return nc.dram_tensor(name, list(shape), F32, kind="ExternalInput").ap()

    def dout(name, shape):
        return nc.dram_tensor(name, list(shape), F32, kind="ExternalOutput").ap()

    x_d = din("x", [NTOK, D])
    cak_d = din("cak", [4, PAST, D])
    cav_d = din("cav", [4, PAST, D])
    cbk_d = din("cbk", [4, 512, D])
    cbv_d = din("cbv", [4, 512, D])
    g_attn_d = din("g_attn", [2, D])
    wqkv_d = din("w_a_qkv", [D, 3 * D])
    alam_d = din("a_lambda", [256])
    asub_d = din("a_subln", [128])
    wao_d = din("w_a_o", [D, D])
    g_kv_d = din("g_kv", [D])
    wkv_d = din("w_kv", [D, 2 * D])
    wbq_d = din("w_b_q", [D, D])
    brel_d = din("b_rel", [16, 257])
    wbo_d = din("w_b_o", [D, D])
    g_mlp_d = din("g_mlp", [2, D])
    wff1_d = din("w_ff1", [2, D, DFF])
    wff2_d = din("w_ff2", [2, DFF, D])
    g_fin_d = din("g_final", [D])
    cs_d = din("c_cs", [128, NT, 64])
    css_d = din("c_css", [32, 64])
    ident_d = din("c_ident", [128, 128])
    mask_d = din("c_mask", [128, GW])
    epat_d = din("c_epat", [128, 2, 128])

    y_d = dout("y", [NTOK, D])
    ak_d = dout("ak", [NTOK, D])
    av_d = dout("av", [NTOK, D])
    bk_d = dout("bk", [640, D])
    bv_d = dout("bv", [640, D])

    rep_d = nc.dram_tensor("rep_scr", [16 * 128 * LEXT], F32).ap()

    with ExitStack() as es:
        S = Sched(nc, es)

        uniq = [0]

        def sbuf(stack, name, shape, dt):
            uniq[0] += 1
            return stack.enter_context(nc.sbuf_tensor("%s_%d" % (name, uniq[0]), list(shape), dt))

        def psum(name, shape, dt):
            return es.enter_context(nc.psum_tensor(name, list(shape), dt))

        h = sbuf(es, "h", [128, NT, D], F32)
        A = sbuf(es, "A", [128, 8, NTOK], BF16)
        cs = sbuf(es, "cs", [128, NT, 64], F32)
        css = sbuf(es, "css", [32, 64], F32)
        ident = sbuf(es, "ident", [128, 128], BF16)
        ones = sbuf(es, "ones", [128, 128], BF16)
        onesf = sbuf(es, "onesf", [128, 128], F32)
        epat = sbuf(es, "epat", [128, 2, 128], BF16)
        ss = sbuf(es, "ss", [128, NT], F32)
        rstd = sbuf(es, "rstd", [128, NT], F32)
        lamb = sbuf(es, "lamb", [128, 256], F32)
        lamt = sbuf(es, "lamt", [128, 128], F32)
        lams = sbuf(es, "lams", [128, 4], F32)
        neglam = sbuf(es, "neglam", [128, 1], F32)
        gsub = sbuf(es, "gsub", [128, 1], F32)

        PS_S = [psum("pss0", [128, 2, 512], F32), psum("pss1", [128, 2, 512], F32)]
        PS_O = psum("pso", [128, 4, 512], F32)

        def sbank(i, b):
            return PS_S[i][:, b, :]

        def sbank16(i, b):
            return PS_S[i][:, b, :].bitcast(BF16)

        def obank(k):
            return PS_O[:, k, :]

        def KS(i, b):
            return ("S", i, b)

        def KO(k):
            return ("O", k)

        S.dma("sp", cs[:], cs_d, writes=["cs"])
        S.dma("sp", css[:], css_d, writes=["css"])
        S.dma("pool", ident[:], ident_d, writes=["ident"])
        S.dma("pool", epat[:], epat_d, writes=["epat"])
        S.op("dve", lambda e: e.memset(ones[:], 1.0), writes=["ones"])
        S.op("dve", lambda e: e.memset(onesf[:], 1.0), writes=["onesf"])
        for t in range(NT):
            S.dma("sp", h[:, t, :], x_d[t * 128:(t + 1) * 128, :], writes=[("h", t)])

        lam0 = 0.8 - 0.6 * math.exp(-0.3 * 0)
        S.dma("sp", lamb[:], alam_d.partition_broadcast(128), writes=["lamb"])
        S.op("dve", lambda e: e.tensor_tensor(lamt[:, 0:64], lamb[:, 0:64], lamb[:, 64:128], ALU.mult),
             reads=["lamb"], writes=["lamt0"])
        S.op("dve", lambda e: e.tensor_tensor(lamt[:, 64:128], lamb[:, 128:192], lamb[:, 192:256], ALU.mult),
             reads=["lamb"], writes=["lamt1"])
        S.op("dve", lambda e: e.reduce_sum(lams[:, 0:1], lamt[:, 0:64], AX.X), reads=["lamt0"], writes=["lams0"])
        S.op("dve", lambda e: e.reduce_sum(lams[:, 1:2], lamt[:, 64:128], AX.X), reads=["lamt1"], writes=["lams1"])
        S.op("act", lambda e: e.activation(out=lams[:, 2:4], in_=lams[:, 0:2], func=AF.Exp),
             reads=["lams0", "lams1"], writes=["lams2"])
        S.op("dve", lambda e: e.tensor_tensor(neglam[:], lams[:, 3:4], lams[:, 2:3], ALU.subtract),
             reads=["lams2"], writes=["neglam"])
        S.op("dve", lambda e: e.tensor_scalar(neglam[:], neglam[:], -lam0, 1.0, ALU.add, ALU.mult),
             reads=["neglam"], writes=["neglam"])
        S.dma("sp", gsub[:], asub_d.rearrange("(p o) -> p o", o=1), writes=["gsub"])
        S.op("dve", lambda e: e.tensor_scalar(gsub[:], gsub[:], 1.0 - lam0, 0.0, ALU.mult, ALU.add),
             reads=["gsub"], writes=["gsub"])

        def relbias_prep(stack):
            ext = sbuf(stack, "ext", [16, LEXT], F32)
            S.dma("sp", ext[:, 0:257], brel_d, writes=["ext"])
            S.op("dve", lambda e: e.tensor_copy(ext[:, 257:LEXT], ext[:, 256:257].broadcast_to([16, LEXT - 257])),
                 reads=["ext"], writes=["ext"])
            rep_v = rep_d.rearrange("(h r m) -> h r m", h=16, r=128)
            S.dma("sp", rep_v, ext[:].unsqueeze(1).broadcast_to([16, 128, LEXT]), reads=["ext"], writes=["rep"])

        def load_w(dst, src2d, key, nsplit=4):
            v = src2d.rearrange("(k p) n -> p k n", p=128)
            step = 8 // nsplit
            for i in range(nsplit):
                S.dma("pool", dst[:, i * step:(i + 1) * step, :], v[:, i * step:(i + 1) * step, :],
                      writes=[(key, i)])
            return [(key, i) for i in range(nsplit)]

        def stats_begin():
            S.op("dve", lambda e: e.memset(ss[:], 0.0), writes=["ss"])

        def stats_tile(t):
            S.op("act", lambda e, t=t, jk=junk: e.activation(out=jk[:], in_=h[:, t, :], func=AF.Square,
                                                             accum_out=ss[:, t:t + 1]),
                 reads=[("h", t), "ss"], writes=["junk", ("ss", t)])

        def stats_end():
            S.op("dve", lambda e: e.tensor_scalar(rstd[:], ss[:], 1.0 / D, EPS, ALU.mult, ALU.add),
                 reads=[("ss", t) for t in range(NT)], writes=["rstd"])
            S.op("act", lambda e: e.activation(out=rstd[:], in_=rstd[:], func=AF.Sqrt), reads=["rstd"], writes=["rstd"])
            S.op("dve", lambda e: e.reciprocal(rstd[:], rstd[:]), reads=["rstd"], writes=["rstd"])

        def norm_stats():
            stats_begin()
            for t in range(NT):
                stats_tile(t)
            stats_end()

        def norm_to_A(g_ap):
            S.dma("sp", grow[:], g_ap.partition_broadcast(128), writes=["grow"])
            for t in range(NT):
                xb = xn[t % 2]
                S.op("dve", lambda e, t=t, xb=xb: e.scalar_tensor_tensor(xb[:], h[:, t, :], rstd[:, t:t + 1], grow[:],
                                                                     ALU.mult, ALU.mult),
                     reads=[("h", t), "rstd", "grow"], writes=[("xn", t % 2)])
                pt = sbank16(t % 2, 0)

                def tr(e, xb=xb, pt=pt):
                    ins = None
                    for kc in range(8):
                        ins = e.transpose(pt[:, kc * 128:(kc + 1) * 128], xb[:, kc * 128:(kc + 1) * 128], ident[:])
                    return ins
                S.op("pe", tr, reads=[("xn", t % 2), "ident"], excl=[KS(t % 2, 0)])
                S.op("act", lambda e, t=t, pt=pt: e.activation(out=A[:, :, t * 128:(t + 1) * 128],
                                                               in_=pt.rearrange("p (k c) -> p k c", k=8), func=AF.Copy),
                     writes=[("A", t)], excl=[KS(t % 2, 0)])

        def Akeys(t0, t1):
            return [("A", t) for t in range(t0, t1)]

        def rope(src4, dst4, cosb, sinb, tc, ts, shape4, rkeys, wkeys, excl, tkey):
            S.op("dve", lambda e: e.tensor_tensor(tc, src4, cosb, ALU.mult), reads=rkeys, writes=[tkey + "c"], excl=excl)
            S.op("dve", lambda e: e.tensor_tensor(ts, src4, sinb, ALU.mult), reads=rkeys, writes=[tkey + "s"], excl=excl)
            S.op("pool", lambda e: e.tensor_tensor(dst4[:, :, 0, :], tc[:, :, 0, :], ts[:, :, 1, :], ALU.subtract),
                 reads=[tkey + "c", tkey + "s"], writes=wkeys)
            S.op("pool", lambda e: e.tensor_tensor(dst4[:, :, 1, :], tc[:, :, 1, :], ts[:, :, 0, :], ALU.add),
                 reads=[tkey + "c", tkey + "s"], writes=wkeys)

        def kv_slots():
            return [(PS_S[0][:, 0, :], KS(0, 0), PS_S[0][:, 1, :], KS(0, 1)),
                    (PS_S[1][:, 0, :], KS(1, 0), PS_S[1][:, 1, :], KS(1, 1)),
                    (obank(0), KO(0), obank(1), KO(1)),
                    (obank(2), KO(2), obank(3), KO(3))]

        def kv_phase(layer, u, Wkv, wkeys, out_k, out_v):
            slots = kv_slots()
            items = [("p", t) for t in range(NT - 1)] + [("s", s) for s in range(4)]

            def stage_a(idx):
                kind, t = items[idx]
                q = idx % 4
                bkv, kkv, btr, ktr = slots[q]
                P = 128 if kind == "p" else 32
                c0 = t * 128 if kind == "p" else SEQ + 32 * t
                pk = bkv[0:P, 0:256]

                def mm(e, c0=c0, P=P, pk=pk):
                    ins = None
                    for kc in range(8):
                        ins = e.matmul(pk, A[:, kc, c0:c0 + P], Wkv[:, kc, :], start=(kc == 0), stop=(kc == 7))
                    return ins
                S.op("pe", mm, reads=[("A", t if kind == "p" else 16)] + wkeys, excl=[kkv])
                kst = kstage[q % len(kstage)][0:P, :]
                qk = q % len(kstage)
                vst = vstage[q][0:P, :]
                k16q = k16[q % len(k16)][0:P, :]
                if layer == 0:
                    if kind == "p":
                        cosb = cs[:, t, 0:32].unsqueeze(1).unsqueeze(1).broadcast_to([128, 2, 2, 32])
                        sinb = cs[:, t, 32:64].unsqueeze(1).unsqueeze(1).broadcast_to([128, 2, 2, 32])
                        ck = "cs"
                    else:
                        cosb = css[:, 0:32].unsqueeze(1).unsqueeze(1).broadcast_to([32, 2, 2, 32])
                        sinb = css[:, 32:64].unsqueeze(1).unsqueeze(1).broadcast_to([32, 2, 2, 32])
                        ck = "css"
                    r4 = "p (g c f) -> p g c f"
                    rope(pk[:, 0:128].rearrange(r4, c=2, f=32), kst.rearrange(r4, c=2, f=32), cosb, sinb,
                         ktc[q][0:P, :].rearrange(r4, c=2, f=32), kts[q][0:P, :].rearrange(r4, c=2, f=32), None,
                         [ck], [("kst", qk)], [kkv], "kt%d" % q)
                else:
                    S.op("act", lambda e, kst=kst, pk=pk: e.activation(out=kst, in_=pk[:, 0:128], func=AF.Copy),
                         writes=[("kst", qk)], excl=[kkv])
                S.op("act", lambda e, vst=vst, pk=pk: e.activation(out=vst, in_=pk[:, 128:256], func=AF.Copy),
                     writes=[("vst", q)], excl=[kkv])
                S.op("act", lambda e, kst=kst, k16q=k16q: e.activation(out=k16q, in_=kst, func=AF.Copy),
                     reads=[("kst", qk)], writes=[("k16", q)])
                vt = V[:, t] if kind == "p" else Vs[:, t]
                if layer == 0:
                    S.op("pool", lambda e, vt=vt, vst=vst: e.tensor_copy(vt[:, 0, :], vst),
                         reads=[("vst", q)], writes=["V" if kind == "p" else "Vs"])
                else:
                    base = vt[:, 0, 0:1]
                    vdst = bass.AP(base.tensor, base.offset, [[base.ap[0][0], P], [192, 2], [1, 64]])
                    S.op("pool", lambda e, vdst=vdst, vst=vst: e.tensor_copy(vdst, vst.rearrange("p (a e) -> p a e", a=2)),
                         reads=[("vst", q)], writes=["V" if kind == "p" else "Vs"])
                if kind == "p":
                    need_out = (layer == 0) or (t >= 12)
                    r0 = t * 128 if layer == 0 else (t - 12) * 128
                else:
                    need_out = True
                    r0 = (SEQ if layer == 0 else 512) + 32 * t
                if need_out:
                    S.dma("sp", out_k[r0:r0 + P, u * 128:(u + 1) * 128], kst, reads=[("kst", qk)])
                    S.dma("sp", out_v[r0:r0 + P, u * 128:(u + 1) * 128], vst, reads=[("vst", q)])

            def stage_b(idx):
                kind, t = items[idx]
                q = idx % 4
                bkv, kkv, btr, ktr = slots[q]
                P = 128 if kind == "p" else 32
                c0 = t * 128 if kind == "p" else SEQ + 32 * t
                k16q = k16[q][0:P, :]
                ptk = btr.bitcast(BF16)
                S.op("pe", lambda e, ptk=ptk, k16q=k16q, P=P: e.transpose(ptk[:, 0:P], k16q, ident[0:P, 0:P]),
                     reads=[("k16", q), "ident"], excl=[ktr])
                S.op("dve", lambda e, c0=c0, P=P, ptk=ptk: e.tensor_copy(KT[:, c0:c0 + P], ptk[:, 0:P]),
                     writes=["KT"], excl=[ktr])

            SKEW = 3
            for idx in range(len(items) + SKEW):
                if idx < len(items):
                    stage_a(idx)
                if idx - SKEW >= 0:
                    stage_b(idx - SKEW)

        def finalize_head(layer, u, ncol, col0, bO, bL, tag, tb):
            outB = B[:, u, col0:col0 + ncol]
            tb = tb % len(T1)
            t1 = T1[tb][:, 0:ncol]
            k1 = ("T1", tb)
            if layer == 0:
                t2 = T2[tb][:, 0:ncol]
                k2 = ("T2", tb)
                o1, o2, l1, l2 = (obank(k)[:, 0:ncol] for k in (0, 1, 2, 3))
                t3 = T3[:, 0:ncol]
                t4 = T4[:, 0:ncol]
                S.op("dve", lambda e: e.tensor_copy(t3, o1), writes=["T3"], excl=[KO(0)])
                S.op("dve", lambda e: e.tensor_copy(t4, o2), writes=["T4"], excl=[KO(1)])
                S.op("act", lambda e: e.activation(out=t1, in_=l1, func=AF.Copy), writes=[k1], excl=[KO(2)])
                S.op("act", lambda e: e.activation(out=t2, in_=l2, func=AF.Copy), writes=[k2], excl=[KO(3)])
                S.op("dve", lambda e: e.tensor_tensor(t3, t3, t2, ALU.mult), reads=[k2, "T3"], writes=["T3"])
                S.op("dve", lambda e: e.tensor_tensor(t4, t4, t1, ALU.mult), reads=[k1, "T4"], writes=["T4"])
                S.op("dve", lambda e: e.scalar_tensor_tensor(t3, t4, neglam[:, 0:1], t3, ALU.mult, ALU.add),
                     reads=["T3", "T4", "neglam"], writes=["T3"])
                S.op("pool", lambda e: e.tensor_tensor(t4, t3, t3, ALU.mult), reads=["T3"], writes=["T4"])
                S.op("dve", lambda e: e.tensor_tensor(t1, t1, t2, ALU.mult), reads=[k1, k2], writes=[k1])
                S.op("dve", lambda e: e.scalar_tensor_tensor(t1, t1, EPS, t1, ALU.mult, ALU.mult), reads=[k1], writes=[k1])

                def tail(psq, kpsq):
                    S.op("pe", lambda e: e.matmul(psq[:, 0:ncol], onesf[:], t4, start=True, stop=True),
                         reads=["T4", "onesf"], excl=[kpsq])
                    S.op("dve", lambda e: e.scalar_tensor_tensor(t2, psq[:, 0:ncol], 1.0 / 128, t1, ALU.mult, ALU.add),
                         reads=[k1], writes=[k2], excl=[kpsq])
                    S.op("act", lambda e: e.activation(out=t2, in_=t2, func=AF.Ln), reads=[k2], writes=[k2])
                    S.op("act", lambda e: e.activation(out=t2, in_=t2, func=AF.Exp, scale=-0.5), reads=[k2], writes=[k2])
                    S.op("dve", lambda e: e.scalar_tensor_tensor(outB, t3, gsub[:, 0:1], t2, ALU.mult, ALU.mult),
                         reads=["T3", k2, "gsub"], writes=[("B", u, tag)])
                return tail
            o = obank(bO)[:, 0:ncol]
            l = obank(bL)[:, 0:ncol]
            t5 = T5[:, 0:ncol]
            S.op("dve", lambda e: e.tensor_copy(t5, o), writes=["T5"], excl=[KO(bO)])
            S.op("act", lambda e: e.activation(out=t1, in_=l, func=AF.Ln), writes=[k1], excl=[KO(bL)])
            S.op("act", lambda e: e.activation(out=t1, in_=t1, func=AF.Exp, scale=-1.0), reads=[k1], writes=[k1])
            S.op("dve", lambda e: e.tensor_tensor(outB, t5, t1, ALU.mult), reads=[k1, "T5"],
                 writes=[("B", u, tag)])
            return None

        def attention_prompt(layer, u):
            steps = []
            for j in range(4):
                if layer == 0:
                    kbs = list(range(0, 4 * j + 4))
                else:
                    kbs = list(range(max(0, 4 * j - 4), 4 * j + 4))
                    first = 4 * j - 2 if j >= 1 else 0
                    kbs.remove(first)
                    kbs.insert(0, first)
                for n, kb in enumerate(kbs):
                    st = dict(j=j, kb=kb, first=(n == 0), last=(n == len(kbs) - 1), diag=False, u0=0)
                    if layer == 0:
                        i = kb - 4 * j
                        st["c0"] = 128 * i if i >= 0 else 0
                        st["c1"] = 512
                        st["diag"] = i >= 0
                    else:
                        u0 = 512 * j - 128 * kb
                        st["u0"] = u0
                        st["c0"] = max(0, -u0)
                        st["c1"] = min(512, GW - u0)
                    steps.append(st)
            for n, st in enumerate(steps):
                st["n"] = n
            if layer == 0:
                sbufs = [(PS_S[0], [KS(0, 0), KS(0, 1)]), (PS_S[1], [KS(1, 0), KS(1, 1)])]
                look = 1
            else:
                sbufs = [(PS_S[0], [KS(0, 0), KS(0, 1)]), (PS_S[1], [KS(1, 0), KS(1, 1)]),
                         (PS_O[:, 2:4, :], [KO(2), KO(3)])]
                look = 2
            nsb = len(sbufs)
            npt = len(PT)

            def emit_qk(st):
                n, j, kb, c0, c1 = st["n"], st["j"], st["kb"], st["c0"], st["c1"]
                ps, pkeys = sbufs[n % nsb]
                pt = PT[n % npt]

                def mm(e):
                    ins = None
                    for a in range(2):
                        ins = e.matmul(ps[:, a, c0:c1], KT[64 * a:64 * a + 64, kb * 128:(kb + 1) * 128],
                                       B[64 * a:64 * a + 64, u, j * 512 + c0:j * 512 + c1], start=True, stop=True)
                    return ins
                S.op("pe", mm, reads=["KT", ("B", u, j)], excl=pkeys)
                for a in range(2):
                    kpt = ("PT", n % npt, a)
                    if layer == 0:
                        S.op("act", lambda e, a=a: e.activation(out=pt[:, a, c0:c1], in_=ps[:, a, c0:c1], func=AF.Exp, scale=0.125),
                             writes=[kpt], excl=[pkeys[a]])
                        if st["diag"]:
                            S.op("dve", lambda e, a=a: e.memset(pt[64:128, a, c0:c0 + 64], 0.0), writes=[kpt])
                    else:
                        u0 = st["u0"]
                        sb_ = SB[n % 2]
                        S.op("dve", lambda e, a=a, sb_=sb_, u0=u0: e.scalar_tensor_tensor(
                            sb_[:, a, c0:c1], ps[:, a, c0:c1], 0.125, G[:, a, u0 + c0:u0 + c1], ALU.mult, ALU.add),
                            reads=["G"], writes=[("SB", n % 2, a)], excl=[pkeys[a]])
                        S.op("act", lambda e, a=a, sb_=sb_: e.activation(out=pt[:, a, c0:c1], in_=sb_[:, a, c0:c1], func=AF.Exp),
                             reads=[("SB", n % 2, a)], writes=[kpt])

            def emit_pv(st):
                n, j, kb, c0, c1 = st["n"], st["j"], st["kb"], st["c0"], st["c1"]
                pt = PT[n % npt]
                first, last = st["first"], st["last"]
                for a in range(2):
                    kpt = ("PT", n % npt, a)
                    if layer == 0:
                        def mm(e, a=a):
                            e.matmul(obank(a)[:, c0:c1], V[:, kb, 0, :], pt[:, a, c0:c1], start=first, stop=last)
                            return e.matmul(obank(2 + a)[:, c0:c1], ones[:], pt[:, a, c0:c1], start=first, stop=last)
                        S.op("pe", mm, reads=[kpt, "V", "ones"], excl=[KO(a), KO(2 + a)])
                    else:
                        def mm(e, a=a):
                            e.matmul(obank(0)[:, c0:c1], V[:, kb, a, :], pt[:, a, c0:c1],
                                     start=(first and a == 0), stop=(last and a == 1))
                            return e.matmul(obank(1)[:, c0:c1], epat[:, a, :], pt[:, a, c0:c1],
                                            start=(first and a == 0), stop=(last and a == 1))
                        S.op("pe", mm, reads=[kpt, "V", "epat"], excl=[KO(0), KO(1)])

            for i in range(min(look, len(steps))):
                emit_qk(steps[i])
            pending = None
            for n, st in enumerate(steps):
                if n + look < len(steps):
                    emit_qk(steps[n + look])
                emit_pv(st)
                if pending is not None and (n >= pending[1] or n == len(steps) - 1):
                    ps, pkeys = sbufs[n % nsb]
                    pending[0](ps[:, 0, :], pkeys[0])
                    pending = None
                if st["last"]:
                    j = st["j"]
                    tail = finalize_head(layer, u, 512, j * 512, 0, 1, j, j % 2)
                    if tail is not None:
                        if n == len(steps) - 1:
                            return tail
                        pending = (tail, n + 3)
            return None

        def sample_load(layer, u, s):
            nblk = 8 if layer == 0 else 4
            ck_d, cv_d = (cak_d, cav_d) if layer == 0 else (cbk_d, cbv_d)
            w = s % 2
            kv = ck_d[s, :, u * 128:(u + 1) * 128].rearrange("(b p) d -> p b d", p=128)
            S.dma("pool", CK[w][:, 0:nblk, :], kv, writes=[("CK", w)])
            if layer == 0:
                vv = cv_d[s, :, u * 128:(u + 1) * 128].rearrange("(b p) d -> p b d", p=128)
                S.dma("pool", CV[w][:, 0:nblk, 0, :], vv, writes=[("CV", w)])
            else:
                for a in range(2):
                    vv = cv_d[s, :, u * 128 + 64 * a:u * 128 + 64 * a + 64].rearrange("(b p) d -> p b d", p=128)
                    S.dma("pool", CV[w][:, 0:nblk, a, 64 * a:64 * a + 64], vv, writes=[("CV", w)])

        def attention_sample(layer, u, last_tail):
            nblk = 8 if layer == 0 else 4
            ncs = nblk * 32

            def st1(s):
                w = s % 2
                pb = s % 2
                ck, ckt = CK[w], CKT[w]
                ptk = sbank16(pb, 0)

                def tr(e):
                    ins = None
                    for b in range(nblk):
                        ins = e.transpose(ptk[:, b * 128:(b + 1) * 128], ck[:, b, :], ident[:])
                    return ins
                S.op("pe", tr, reads=[("CK", w), "ident"], excl=[KS(pb, 0)])
                S.op("act", lambda e: e.activation(out=ckt[:, 0:nblk * 128].rearrange("p (k c) -> p k c", c=128),
                                                   in_=ptk[:, 0:nblk * 128].rearrange("p (k c) -> p k c", c=128), func=AF.Copy),
                     writes=[("CKT", w)], excl=[KS(pb, 0)])

            def st2(s):
                w = s % 2
                pb = s % 2
                c0 = SEQ + 32 * s
                ckt = CKT[w]
                bS = [sbank(pb, 1), sbank(pb, 0)]

                def mmS(e):
                    ins = None
                    for a in range(2):
                        for b in range(nblk):
                            slot = b if layer == 0 else nblk - 1 - b
                            ins = e.matmul(bS[a][:, slot * 32:slot * 32 + 32],
                                           ckt[64 * a:64 * a + 64, b * 128:(b + 1) * 128],
                                           B[64 * a:64 * a + 64, u, c0:c0 + 32], start=True, stop=True)
                        ins = e.matmul(bS[a][0:32, 256:288], KT[64 * a:64 * a + 64, c0:c0 + 32],
                                       B[64 * a:64 * a + 64, u, c0:c0 + 32], start=True, stop=True)
                    return ins
                S.op("pe", mmS, reads=[("CKT", w), "KT", ("B", u, 4)], excl=[KS(pb, 0), KS(pb, 1)])
                pts, ptn = PTs[w], PTn[w]
                for a in range(2):
                    ka = KS(pb, 1 - a)
                    if layer == 0:
                        S.op("act", lambda e, a=a: e.activation(out=pts[:, a, 0:ncs], in_=bS[a][:, 0:ncs], func=AF.Exp, scale=0.125),
                             writes=[("PTs", w)], excl=[ka])
                        S.op("act", lambda e, a=a: e.activation(out=ptn[:, a, :], in_=bS[a][0:32, 256:288], func=AF.Exp, scale=0.125),
                             writes=[("PTn", w)], excl=[ka])
                    else:
                        sb_ = SB[w]
                        gs = G[:, a, 128:640].rearrange("p (s x) -> p s x", x=128)[:, :, 0:32]
                        S.op("dve", lambda e, a=a, gs=gs, sb_=sb_: e.scalar_tensor_tensor(
                            sb_[:, a, 0:ncs].rearrange("p (s x) -> p s x", x=32),
                            bS[a][:, 0:ncs].rearrange("p (s x) -> p s x", x=32), 0.125, gs, ALU.mult, ALU.add),
                            reads=["G"], writes=[("SB", w)], excl=[ka])
                        S.op("dve", lambda e, a=a, sb_=sb_: e.scalar_tensor_tensor(
                            sb_[0:32, a, 256:288], bS[a][0:32, 256:288], 0.125, G[0:32, a, 0:32], ALU.mult, ALU.add),
                            reads=["G"], writes=[("SB", w)], excl=[ka])
                        S.op("act", lambda e, a=a, sb_=sb_: e.activation(out=pts[:, a, 0:ncs], in_=sb_[:, a, 0:ncs], func=AF.Exp),
                             reads=[("SB", w)], writes=[("PTs", w)])
                        S.op("act", lambda e, a=a, sb_=sb_: e.activation(out=ptn[:, a, :], in_=sb_[0:32, a, 256:288], func=AF.Exp),
                             reads=[("SB", w)], writes=[("PTn", w)])

            def st3(s):
                w = s % 2
                cv, pts, ptn = CV[w], PTs[w], PTn[w]

                def mmPV(e):
                    ins = None
                    cs0 = s * 32
                    for b in range(nblk):
                        slot = b if layer == 0 else nblk - 1 - b
                        for a in range(2):
                            rhs = pts[:, a, slot * 32:slot * 32 + 32]
                            if layer == 0:
                                e.matmul(obank(a)[:, cs0:cs0 + 32], cv[:, b, 0, :], rhs, start=(b == 0), stop=False)
                                ins = e.matmul(obank(2 + a)[:, cs0:cs0 + 32], ones[:], rhs, start=(b == 0), stop=False)
                            else:
                                st0 = (b == 0 and a == 0)
                                e.matmul(obank(0)[:, cs0:cs0 + 32], cv[:, b, a, :], rhs, start=st0, stop=False)
                                ins = e.matmul(obank(1)[:, cs0:cs0 + 32], epat[:, a, :], rhs, start=st0, stop=False)
                    for a in range(2):
                        rhs = ptn[0:32, a, :]
                        if layer == 0:
                            e.matmul(obank(a)[:, cs0:cs0 + 32], Vs[0:32, s, 0, :], rhs, start=False, stop=True)
                            ins = e.matmul(obank(2 + a)[:, cs0:cs0 + 32], ones[0:32, :], rhs, start=False, stop=True)
                        else:
                            e.matmul(obank(0)[:, cs0:cs0 + 32], Vs[0:32, s, a, :], rhs, start=False, stop=(a == 1))
                            ins = e.matmul(obank(1)[:, cs0:cs0 + 32], epat[0:32, a, :], rhs, start=False, stop=(a == 1))
                    return ins
                ex = [KO(0), KO(1), KO(2), KO(3)] if layer == 0 else [KO(0), KO(1)]
                S.op("pe", mmPV, reads=[("PTs", w), ("PTn", w), ("CV", w), "Vs", "ones", "epat"], excl=ex)
                if s + 2 < 4:
                    sample_load(layer, u, s + 2)

            st1(0)
            st1(1)
            st2(0)
            if last_tail is not None:
                last_tail(sbank(1, 1), KS(1, 1))
            st2(1); st3(0); st1(2); st2(2); st3(1); st1(3); st2(3); st3(2); st3(3)
            tail = finalize_head(layer, u, 128, SEQ, 0, 1, 4, 0)
            if tail is not None:
                tail(sbank(0, 0), KS(0, 0))

        def out_proj(W, wkeys):
            stats_begin()
            for t in range(NT):
                pb = t % 2
                ps = PS_S[pb]
                jt = min(t // 4, 4)

                def mm(e, t=t, ps=ps):
                    ins = None
                    for nh in range(2):
                        for kc in range(8):
                            ins = e.matmul(ps[:, nh, :], B[:, kc, t * 128:(t + 1) * 128], W[:, kc, nh * 512:(nh + 1) * 512],
                                           start=(kc == 0), stop=(kc == 7))
                    return ins
                S.op("pe", mm, reads=[("B", kc, jt) for kc in range(8)] + wkeys, excl=[KS(pb, 0), KS(pb, 1)])
                S.op("dve", lambda e, t=t, ps=ps: e.tensor_tensor(h[:, t, :], ps[:].rearrange("p a n -> p (a n)"), h[:, t, :], ALU.add),
                     reads=[("h", t)], writes=[("h", t)], excl=[KS(pb, 0), KS(pb, 1)])
                stats_tile(t)
            stats_end()

        def mlp(l):
            norm_to_A(g_mlp_d[l])
            if l == 0:
                relbias_prep(es_m)
            stats_begin()
            blocks = [(0, 512), (512, 512), (1024, 512), (1536, 512), (2048, 128)]
            wkeys = {}

            def load_group(fg):
                k1 = load_w(W1b[fg % 2], wff1_d[l][:, fg * 1024:(fg + 1) * 1024], ("W1", fg % 2))
                k2 = load_w(W2b[fg % 2], wff2_d[l][fg * 1024:(fg + 1) * 1024, :], ("W2", fg % 2))
                wkeys[fg] = (k1, k2)

            def up(fg, bi, seq):
                W1 = W1b[fg % 2]
                k1 = wkeys[fg][0]
                c0, n = blocks[bi]
                Hb = H1[seq % 2]
                for fc in range(8):
                    hb = fc % 4
                    ph = obank(hb)[:, 0:n]

                    def mm(e, fc=fc, ph=ph):
                        ins = None
                        for kc in range(8):
                            ins = e.matmul(ph, W1[:, kc, fc * 128:(fc + 1) * 128], A[:, kc, c0:c0 + n],
                                           start=(kc == 0), stop=(kc == 7))
                        return ins
                    S.op("pe", mm, reads=Akeys(c0 // 128, (c0 + n) // 128) + k1, excl=[KO(hb)])
                    rl = RL[fc % 2]
                    S.op("act", lambda e, ph=ph, rl=rl: e.activation(out=rl[:, 0:n], in_=ph, func=AF.Relu),
                         writes=[("RL", fc % 2)], excl=[KO(hb)])
                    S.op("pool", lambda e, fc=fc, rl=rl: e.tensor_tensor(Hb[:, fc, 0:n], rl[:, 0:n], rl[:, 0:n], ALU.mult),
                         reads=[("RL", fc % 2)], writes=[("H1", seq % 2, fc)])

            def down(fg, bi, seq):
                W2 = W2b[fg % 2]
                k2 = wkeys[fg][1]
                c0, n = blocks[bi]
                Hb = H1[seq % 2]
                for tt in range(n // 128):
                    t = c0 // 128 + tt
                    pb = t % 2
                    ps = PS_S[pb]

                    def mm2(e, tt=tt, ps=ps):
                        ins = None
                        for nh in range(2):
                            for fc in range(8):
                                ins = e.matmul(ps[:, nh, :], Hb[:, fc, tt * 128:(tt + 1) * 128],
                                               W2[:, fc, nh * 512:(nh + 1) * 512], start=(fc == 0), stop=(fc == 7))
                        return ins
                    S.op("pe", mm2, reads=[("H1", seq % 2, fc) for fc in range(8)] + k2, excl=[KS(pb, 0), KS(pb, 1)])
                    S.op("dve", lambda e, t=t, ps=ps: e.tensor_tensor(h[:, t, :], ps[:].rearrange("p a n -> p (a n)"),
                                                                  h[:, t, :], ALU.add),
                         reads=[("h", t)], writes=[("h", t)], excl=[KS(pb, 0), KS(pb, 1)])
                    if fg == 3:
                        stats_tile(t)

            work = [(fg, bi) for fg in range(4) for bi in range(len(blocks))]
            load_group(0)
            for i, (fg, bi) in enumerate(work):
                if bi == 1 and fg + 1 < 4:
                    load_group(fg + 1)
                if i == 0:
                    up(fg, bi, i)
                if i + 1 < len(work):
                    up(work[i + 1][0], work[i + 1][1], i + 1)
                down(fg, bi, i)
            stats_end()

        for layer in range(2):
            with ExitStack() as es_l:
                B = sbuf(es_l, "B%d" % layer, [128, 8, NTOK], BF16)
                with ExitStack() as es_q:
                    grow = sbuf(es_q, "grow", [128, D], F32)
                    xn = [sbuf(es_q, "xn0", [128, D], BF16), sbuf(es_q, "xn1", [128, D], BF16)]
                    junk = sbuf(es_q, "junk", [128, D], BF16)
                    Wq = sbuf(es_q, "Wq", [128, 8, D], BF16)
                    if layer == 0:
                        norm_stats()
                        tcq = [sbuf(es_q, "tcq%d" % i, [128, D], F32) for i in range(2)]
                        tsq = [sbuf(es_q, "tsq%d" % i, [128, D], F32) for i in range(2)]
                        qr = [sbuf(es_q, "qr%d" % i, [128, D], BF16) for i in range(2)]
                        wk = load_w(Wq, wqkv_d[:, 0:D], "Wq")
                        norm_to_A(g_attn_d[0])
                        r4 = "p (g c f) -> p g c f"

                        def q_a(t):
                            pb = t % 2
                            ps = PS_S[pb]

                            def mm(e):
                                ins = None
                                for nh in range(2):
                                    for kc in range(8):
                                        ins = e.matmul(ps[:, nh, :], A[:, kc, t * 128:(t + 1) * 128],
                                                       Wq[:, kc, nh * 512:(nh + 1) * 512], start=(kc == 0), stop=(kc == 7))
                                return ins
                            S.op("pe", mm, reads=[("A", t)] + wk, excl=[KS(pb, 0), KS(pb, 1)])
                            cosb = cs[:, t, 0:32].unsqueeze(1).unsqueeze(1).broadcast_to([128, 16, 2, 32])
                            sinb = cs[:, t, 32:64].unsqueeze(1).unsqueeze(1).broadcast_to([128, 16, 2, 32])
                            src4 = ps[:].rearrange("p a (g c f) -> p (a g) c f", c=2, f=32)
                            tc = tcq[pb][:].rearrange(r4, c=2, f=32)
                            ts = tsq[pb][:].rearrange(r4, c=2, f=32)
                            dst4 = qr[pb][:].rearrange(r4, c=2, f=32)
                            ex = [KS(pb, 0), KS(pb, 1)]
                            S.op("dve", lambda e: e.tensor_tensor(tc, src4, cosb, ALU.mult), reads=["cs"], writes=[("tqc", pb)], excl=ex)
                            S.op("dve", lambda e: e.tensor_tensor(ts, src4, sinb, ALU.mult), reads=["cs"], writes=[("tqs", pb)], excl=ex)
                            S.op("pool", lambda e: e.tensor_tensor(dst4[:, :, 0, :], tc[:, :, 0, :], ts[:, :, 1, :], ALU.subtract),
                                 reads=[("tqc", pb), ("tqs", pb)], writes=[("qr0", pb)])
                            S.op("dve", lambda e: e.tensor_tensor(dst4[:, :, 1, :], tc[:, :, 1, :], ts[:, :, 0, :], ALU.add),
                                 reads=[("tqc", pb), ("tqs", pb)], writes=[("qr1", pb)])

                        def q_b(t):
                            pb = t % 2
                            pt = obank(t % 4).bitcast(BF16)
                            qrt = qr[pb]

                            def tr(e):
                                ins = None
                                for kc in range(8):
                                    ins = e.transpose(pt[:, kc * 128:(kc + 1) * 128], qrt[:, kc * 128:(kc + 1) * 128], ident[:])
                                return ins
                            S.op("pe", tr, reads=[("qr0", pb), ("qr1", pb), "ident"], excl=[KO(t % 4)])
                            jt = min(t // 4, 4)
                            S.op("act", lambda e: e.activation(out=B[:, :, t * 128:(t + 1) * 128],
                                                               in_=pt.rearrange("p (k c) -> p k c", k=8), func=AF.Copy),
                                 writes=[("B", kc, jt) for kc in range(8)], excl=[KO(t % 4)])
                        for t in range(NT + 1):
                            if t < NT:
                                q_a(t)
                            if t >= 1:
                                q_b(t - 1)
                    else:
                        wk = load_w(Wq, wbq_d, "Wq")
                        norm_to_A(g_attn_d[1])
                        blocks = [(0, 512), (512, 512), (1024, 512), (1536, 512), (2048, 128)]
                        n_ = 0
                        for c in range(8):
                            for bi, (c0, n) in enumerate(blocks):
                                pk = obank(n_ % 4)[:, 0:n]

                                def mm(e, c=c, c0=c0, n=n, pk=pk):
                                    ins = None
                                    for kc in range(8):
                                        ins = e.matmul(pk, Wq[:, kc, c * 128:(c + 1) * 128], A[:, kc, c0:c0 + n],
                                                       start=(kc == 0), stop=(kc == 7))
                                    return ins
                                S.op("pe", mm, reads=Akeys(c0 // 128, (c0 + n) // 128) + wk, excl=[KO(n_ % 4)])
                                S.op("act", lambda e, c=c, c0=c0, n=n, pk=pk: e.activation(out=B[:, c, c0:c0 + n], in_=pk, func=AF.Copy),
                                     writes=[("B", c, bi)], excl=[KO(n_ % 4)])
                                n_ += 1
                        norm_to_A(g_kv_d)
                    S.end_phase()
                with ExitStack() as es_a:
                    KT = sbuf(es_a, "KT", [128, NTOK], BF16)
                    V = sbuf(es_a, "V", [128, 16, 2, 128], BF16)
                    Vs = sbuf(es_a, "Vs", [32, 4, 2, 128], BF16)
                    nblk_l = 8 if layer == 0 else 4
                    npt_l = 2 if layer == 0 else 3
                    PT = [sbuf(es_a, "PT%d" % i, [128, 2, 512], BF16) for i in range(npt_l)]
                    T1 = [sbuf(es_a, "T1a", [128, 512], F32)]
                    if layer == 0:
                        T1.append(sbuf(es_a, "T1b", [128, 512], F32))
                    if layer == 0:
                        T2 = [sbuf(es_a, "T2a", [128, 512], F32), sbuf(es_a, "T2b", [128, 512], F32)]
                        T3 = sbuf(es_a, "T3", [128, 512], F32)
                        T4 = sbuf(es_a, "T4", [128, 512], F32)
                    CK = [sbuf(es_a, "CK%d" % i, [128, nblk_l, 128], BF16) for i in range(2)]
                    CV = [sbuf(es_a, "CV%d" % i, [128, nblk_l, 2 if layer == 1 else 1, 128], BF16) for i in range(2)]
                    CKT = [sbuf(es_a, "CKT%d" % i, [128, nblk_l * 128], BF16) for i in range(2)]
                    PTs = [sbuf(es_a, "PTs%d" % i, [128, 2, 256], BF16) for i in range(2)]
                    PTn = [sbuf(es_a, "PTn%d" % i, [32, 2, 32], BF16) for i in range(2)]
                    Wkvb = [sbuf(es_a, "Wkv0", [128, 8, 256], BF16), sbuf(es_a, "Wkv1", [128, 8, 256], BF16)]
                    kstage = [sbuf(es_a, "kst%d" % i, [128, 128], F32) for i in range(4 if layer == 0 else 2)]
                    vstage = [sbuf(es_a, "vst%d" % i, [128, 128], F32) for i in range(4)]
                    k16 = [sbuf(es_a, "k16_%d" % i, [128, 128], BF16) for i in range(4)]
                    if layer == 0:
                        ktc = [sbuf(es_a, "ktc%d" % i, [128, 128], F32) for i in range(4)]
                        kts = [sbuf(es_a, "kts%d" % i, [128, 128], F32) for i in range(4)]
                    if layer == 1:
                        T5 = sbuf(es_a, "T5", [128, 512], F32)
                        SB = [sbuf(es_a, "SB%d" % i, [128, 2, 512], F32) for i in range(2)]
                        G = sbuf(es_a, "G", [128, 2, GW], F32)
                        Mk = sbuf(es_a, "Mk", [128, GW], F32)
                        S.dma("sp", Mk[:], mask_d, writes=["Mk"])
                        S.op("pool", lambda e: e.memset(V[:], 0.0), writes=["V"])
                        S.op("pool", lambda e: e.memset(Vs[:], 0.0), writes=["Vs"])
                        for i in range(2):
                            S.op("pool", lambda e, i=i: e.memset(CV[i][:], 0.0), writes=[("CV", i)])
                    out_k, out_v = (ak_d, av_d) if layer == 0 else (bk_d, bv_d)
                    def load_wkv(u):
                        Wkv = Wkvb[u % 2]
                        if layer == 0:
                            srcs = [wqkv_d[:, D + u * 128:D + (u + 1) * 128], wqkv_d[:, 2 * D + u * 128:2 * D + (u + 1) * 128]]
                        else:
                            srcs = [wkv_d[:, u * 128:(u + 1) * 128], wkv_d[:, D + u * 128:D + (u + 1) * 128]]
                        for i, src in enumerate(srcs):
                            S.dma("pool", Wkv[:, :, i * 128:(i + 1) * 128], src.rearrange("(k p) n -> p k n", p=128),
                                  writes=[("Wkv", u % 2, i)])
                    load_wkv(0)
                    for u in range(8):
                        Wkv = Wkvb[u % 2]
                        wkeys = [("Wkv", u % 2, 0), ("Wkv", u % 2, 1)]
                        if layer == 1:
                            for a in range(2):
                                src = bass.AP(rep_d.tensor, (2 * u + a) * 128 * LEXT + 128, [[LEXT - 1, 128], [1, GW]])
                                S.dma("sp", G[:, a, :], src, reads=["rep"], writes=["G"])
                            S.op("pool", lambda e: e.tensor_tensor(G[:], G[:], Mk[:].unsqueeze(1).broadcast_to([128, 2, GW]), ALU.add),
                                 reads=["G", "Mk"], writes=["G"])
                        kv_phase(layer, u, Wkv, wkeys, out_k, out_v)
                        if u + 1 < 8:
                            load_wkv(u + 1)
                        sample_load(layer, u, 0)
                        sample_load(layer, u, 1)
                        last_tail = attention_prompt(layer, u)
                        attention_sample(layer, u, last_tail)
                    S.end_phase()
                with ExitStack() as es_o:
                    Wo = sbuf(es_o, "Wo", [128, 8, D], BF16)
                    junk = sbuf(es_o, "junko", [128, D], BF16)
                    wk = load_w(Wo, wao_d if layer == 0 else wbo_d, "Wo")
                    out_proj(Wo, wk)
                    S.end_phase()
            with ExitStack() as es_m:
                grow = sbuf(es_m, "growm", [128, D], F32)
                xn = [sbuf(es_m, "xnm0", [128, D], BF16), sbuf(es_m, "xnm1", [128, D], BF16)]
                junk = sbuf(es_m, "junkm", [128, D], BF16)
                W1b = [sbuf(es_m, "W1a", [128, 8, D], BF16), sbuf(es_m, "W1b", [128, 8, D], BF16)]
                W2b = [sbuf(es_m, "W2a", [128, 8, D], BF16), sbuf(es_m, "W2b", [128, 8, D], BF16)]
                H1 = [sbuf(es_m, "H1a", [128, 8, 512], BF16), sbuf(es_m, "H1b", [128, 8, 512], BF16)]
                RL = [sbuf(es_m, "RL0", [128, 512], F32), sbuf(es_m, "RL1", [128, 512], F32)]
                mlp(layer)
                S.end_phase()
        with ExitStack() as es_f:
            grow = sbuf(es_f, "growf", [128, D], F32)
            junk = sbuf(es_f, "junkf", [128, D], BF16)
            ys = [sbuf(es_f, "ys%d" % i, [128, D], F32) for i in range(3)]
            S.dma("sp", grow[:], g_fin_d.partition_broadcast(128), writes=["grow"])
            for t in range(NT):
                yb = ys[t % 3]
                S.op("dve",
                     lambda e, t=t, yb=yb: e.scalar_tensor_tensor(yb[:], h[:, t, :], rstd[:, t:t + 1], grow[:],
                                                                  ALU.mult, ALU.mult),
                     reads=[("h", t), "rstd", "grow"], writes=[("ys", t % 3)])
                S.dma("sp", y_d[t * 128:(t + 1) * 128, :], yb[:], reads=[("ys", t % 3)])
            S.end_phase()
    return nc


def _const_tables():
    half = 32
    inv = 1.0 / (10000.0 ** (np.arange(half, dtype=np.float32) / half))
    pos = np.zeros((128, NT), np.float32)
    for t in range(16):
        pos[:, t] = t * 128 + np.arange(128)
    pos[:, 16] = PAST + (np.arange(128) % 32)
    ang = pos[:, :, None] * inv[None, None, :]
    cs = np.concatenate([np.cos(ang), np.sin(ang)], axis=-1).astype(np.float32)
    angs = (PAST + np.arange(32, dtype=np.float32))[:, None] * inv[None, :]
    css = np.concatenate([np.cos(angs), np.sin(angs)], axis=-1).astype(np.float32)
    ident = np.eye(128, dtype=np.float32)
    kl = np.arange(128)[:, None] // 64
    uu = np.arange(GW)[None, :] // 64
    dlt = uu - kl
    mask = np.where((dlt >= 0) & (dlt <= 8), 0.0, NEG).astype(np.float32)
    epat = np.zeros((128, 2, 128), np.float32)
    epat[:, 0, 0:64] = 1.0
    epat[:, 1, 64:128] = 1.0
    return cs, css, ident, mask, epat


_NC_CACHE = {}


def kernel(x_prompt, x_sample, cache_a_k, cache_a_v, cache_b_k, cache_b_v,
           g_attn, w_a_qkv, a_lambda, a_subln, w_a_o, g_kv, w_kv, w_b_q, b_rel, w_b_o,
           g_mlp, w_ff1, w_ff2, g_final):
    f = lambda a: np.ascontiguousarray(np.asarray(a, dtype=np.float32))
    x_prompt, x_sample = f(x_prompt), f(x_sample)
    cache_a_k, cache_a_v, cache_b_k, cache_b_v = f(cache_a_k), f(cache_a_v), f(cache_b_k), f(cache_b_v)
    cs, css, ident, mask, epat = _const_tables()
    shared = {
        "g_attn": f(g_attn), "w_a_qkv": f(w_a_qkv)[0], "a_lambda": f(a_lambda).reshape(256),
        "a_subln": f(a_subln).reshape(128), "w_a_o": f(w_a_o)[0], "g_kv": f(g_kv), "w_kv": f(w_kv),
        "w_b_q": f(w_b_q)[0], "b_rel": f(b_rel)[0], "w_b_o": f(w_b_o)[0], "g_mlp": f(g_mlp),
        "w_ff1": f(w_ff1), "w_ff2": f(w_ff2), "g_final": f(g_final),
        "c_cs": cs, "c_css": css, "c_ident": ident, "c_mask": mask, "c_epat": epat,
    }
    in_maps = []
    for c in range(N_CORES):
        m = dict(shared)
        m["x"] = np.concatenate([x_prompt[c], x_sample[4 * c:4 * c + 4].reshape(128, D)], axis=0)
        m["cak"] = cache_a_k[0, 4 * c:4 * c + 4].reshape(4, PAST, D)
        m["cav"] = cache_a_v[0, 4 * c:4 * c + 4].reshape(4, PAST, D)
        m["cbk"] = cache_b_k[4 * c:4 * c + 4].reshape(4, 512, D)
        m["cbv"] = cache_b_v[4 * c:4 * c + 4].reshape(4, 512, D)
        in_maps.append(m)
    if "nc" not in _NC_CACHE:
        _NC_CACHE["nc"] = build_nc()
    res = run_bass_kernel_spmd(_NC_CACHE["nc"], in_maps, core_ids=list(range(N_CORES)))
    R = res.results
    y_p = np.stack([R[c]["y"][:SEQ] for c in range(N_CORES)])
    y_s = np.concatenate([R[c]["y"][SEQ:].reshape(4, 32, D) for c in range(N_CORES)])
    ak_p = np.stack([R[c]["ak"][:SEQ].reshape(SEQ, 8, 128) for c in range(N_CORES)])[None]
    av_p = np.stack([R[c]["av"][:SEQ].reshape(SEQ, 8, 128) for c in range(N_CORES)])[None]
    ak_s = np.concatenate([R[c]["ak"][SEQ:].reshape(4, 32, 8, 128) for c in range(N_CORES)])[None]
    av_s = np.concatenate([R[c]["av"][SEQ:].reshape(4, 32, 8, 128) for c in range(N_CORES)])[None]
    bk_p = np.stack([R[c]["bk"][:512].reshape(512, 16, 64) for c in range(N_CORES)])
    bv_p = np.stack([R[c]["bv"][:512].reshape(512, 16, 64) for c in range(N_CORES)])
    bk_s = np.concatenate([R[c]["bk"][512:].reshape(4, 32, 16, 64) for c in range(N_CORES)])
    bv_s = np.concatenate([R[c]["bv"][512:].reshape(4, 32, 16, 64) for c in range(N_CORES)])
    outs = (y_p, y_s, ak_p, av_p, bk_p, bv_p, ak_s, av_s, bk_s, bv_s)
    return tuple(np.ascontiguousarray(o, dtype=np.float32) for o in outs)
```

```python
import math
from contextlib import ExitStack

import numpy as np

import concourse.bass as bass
import concourse.mybir as mybir
from concourse.bass_utils import run_bass_kernel_spmd

F32 = mybir.dt.float32
BF16 = mybir.dt.bfloat16
AF = mybir.ActivationFunctionType
ALU = mybir.AluOpType
AX = mybir.AxisListType

D = 1024
SEQ = 2048
NT = 17
NTOK = NT * 128
PAST = 1024
DFF = 4096
EPS = 1e-6
LEXT = 896
GW = 768
NEG = -30000.0
N_CORES = 8


class Sched:
    ENG = ("pe", "act", "dve", "pool", "sp")

    def __init__(self, nc, es, ndma=24):
        self.nc = nc
        self.ops = {e: [] for e in self.ENG}
        self.sem = {e: es.enter_context(nc.semaphore("s_" + e)) for e in self.ENG}
        self.cnt = {e: 0 for e in self.ENG}
        self.dsem = [es.enter_context(nc.semaphore("d%d" % i)) for i in range(ndma)]
        self.dcnt = [0] * ndma
        self.dnext2 = [0, 0]
        self.waited = {e: {} for e in self.ENG}
        self.lastw = {}
        self.readers = {}
        self.excl = {}

    def _deps(self, eng, reads, writes, excl):
        deps = {}

        def add(k, v):
            if k == "pe" and eng == "pe":
                return
            if deps.get(k, 0) < v:
                deps[k] = v
        for b in reads:
            ev = self.lastw.get(b)
            if ev is not None:
                add(*ev)
        for b in writes:
            ev = self.lastw.get(b)
            if ev is not None:
                add(*ev)
            for k, v in self.readers.get(b, {}).items():
                add(k, v)
        for b in excl:
            for k, v in self.excl.get(b, {}).items():
                if k != eng:
                    add(k, v)
        out = []
        w = self.waited[eng]
        for k, v in deps.items():
            if w.get(k, 0) < v:
                w[k] = v
                out.append((k, v))
        return out

    def _semof(self, k):
        return self.sem[k] if isinstance(k, str) else self.dsem[k]

    def _mark(self, ev, reads, writes, excl):
        k, v = ev
        for b in reads:
            r = self.readers.setdefault(b, {})
            if r.get(k, 0) < v:
                r[k] = v
        for b in writes:
            self.lastw[b] = ev
            self.readers[b] = {}
        for b in excl:
            self.excl[b] = {k: v}

    def op(self, eng, fn, reads=(), writes=(), excl=()):
        waits = self._deps(eng, reads, writes, excl)
        self.cnt[eng] += 1
        ev = (eng, self.cnt[eng])
        sem = self.sem[eng]
        waitl = [(self._semof(k), v) for k, v in waits]

        def emit(e):
            for s, v in waitl:
                e.wait_ge(s, v)
            fn(e).then_inc(sem, 1)
        self.ops[eng].append(emit)
        self._mark(ev, reads, writes, excl)

    def dma(self, eng, out, in_, reads=(), writes=()):
        half = len(self.dsem) // 2
        qi = 0 if eng == "pool" else 1
        k = qi * half + self.dnext2[qi]
        self.dnext2[qi] = (self.dnext2[qi] + 1) % half
        waits = self._deps(eng, reads, writes, ())
        prev = self.dcnt[k]
        if prev and self.waited[eng].get(k, 0) < prev:
            self.waited[eng][k] = prev
            waits.append((k, prev))
        self.dcnt[k] += 16
        ev = (k, self.dcnt[k])
        sem = self.dsem[k]
        waitl = [(self._semof(kk), v) for kk, v in waits]

        def emit(e):
            for s, v in waitl:
                e.wait_ge(s, v)
            e.dma_start(out=out, in_=in_).then_inc(sem, 16)
        self.ops[eng].append(emit)
        self._mark(ev, reads, writes, ())

    def barrier(self):
        waitl = []
        for e in self.ENG:
            if self.cnt[e]:
                waitl.append((e, self.cnt[e]))
        for k in range(len(self.dsem)):
            if self.dcnt[k]:
                waitl.append((k, self.dcnt[k]))
        for eng in self.ENG:
            mine = []
            for k, v in waitl:
                if self.waited[eng].get(k, 0) < v:
                    self.waited[eng][k] = v
                    mine.append((self._semof(k), v))

            def emit(e, mine=mine):
                for s, v in mine:
                    e.wait_ge(s, v)
            self.ops[eng].append(emit)
        self.lastw.clear()
        self.readers.clear()
        self.excl.clear()

    def end_phase(self):
        self.barrier()
        self.replay()

    def replay(self):
        ops = self.ops
        self.ops = {e: [] for e in self.ENG}
        with self.nc.Block() as block:
            @block.tensor
            def _(e):
                for f in ops["pe"]:
                    f(e)

            @block.scalar
            def _(e):
                for f in ops["act"]:
                    f(e)

            @block.vector
            def _(e):
                for f in ops["dve"]:
                    f(e)

            @block.gpsimd
            def _(e):
                for f in ops["pool"]:
                    f(e)

            @block.sync
            def _(e):
                for f in ops["sp"]:
                    f(e)


def build_nc():
    nc = bass.Bass("TRN2", target_bir_lowering=False)

    def din(name, shape):
        return nc.dram_tensor(name, list(shape), F32, kind="ExternalInput").ap()

    def dout(name, shape):
        return nc.dram_tensor(name, list(shape), F32, kind="ExternalOutput").ap()

    x_d = din("x", [NTOK, D])
    cak_d = din("cak", [4, PAST, D])
    cav_d = din("cav", [4, PAST, D])
    cbk_d = din("cbk", [4, 512, D])
    cbv_d = din("cbv", [4, 512, D])
    g_attn_d = din("g_attn", [2, D])
    wqkv_d = din("w_a_qkv", [D, 3 * D])
    alam_d = din("a_lambda", [256])
    asub_d = din("a_subln", [128])
    wao_d = din("w_a_o", [D, D])
    g_kv_d = din("g_kv", [D])
    wkv_d = din("w_kv", [D, 2 * D])
    wbq_d = din("w_b_q", [D, D])
    brel_d = din("b_rel", [16, 257])
    wbo_d = din("w_b_o", [D, D])
    g_mlp_d = din("g_mlp", [2, D])
    wff1_d = din("w_ff1", [2, D, DFF])
    wff2_d = din("w_ff2", [2, DFF, D])
    g_fin_d = din("g_final", [D])
    cs_d = din("c_cs", [128, NT, 64])
    css_d = din("c_css", [32, 64])
    ident_d = din("c_ident", [128, 128])
    mask_d = din("c_mask", [128, GW])
    epat_d = din("c_epat", [128, 2, 128])

    y_d = dout("y", [NTOK, D])
    ak_d = dout("ak", [NTOK, D])
    av_d = dout("av", [NTOK, D])
    bk_d = dout("bk", [640, D])
    bv_d = dout("bv", [640, D])

    rep_d = nc.dram_tensor("rep_scr", [16 * 128 * LEXT], F32).ap()

    with ExitStack() as es:
        S = Sched(nc, es)

        uniq = [0]

        def sbuf(stack, name, shape, dt):
            uniq[0] += 1
            return stack.enter_context(nc.sbuf_tensor("%s_%d" % (name, uniq[0]), list(shape), dt))

        def psum(name, shape, dt):
            return es.enter_context(nc.psum_tensor(name, list(shape), dt))

        h = sbuf(es, "h", [128, NT, D], F32)
        A = sbuf(es, "A", [128, 8, NTOK], BF16)
        cs = sbuf(es, "cs", [128, NT, 64], F32)
        css = sbuf(es, "css", [32, 64], F32)
        ident = sbuf(es, "ident", [128, 128], BF16)
        ones = sbuf(es, "ones", [128, 128], BF16)
        onesf = sbuf(es, "onesf", [128, 128], F32)
        epat = sbuf(es, "epat", [128, 2, 128], BF16)
        ss = sbuf(es, "ss", [128, NT], F32)
        rstd = sbuf(es, "rstd", [128, NT], F32)
        lamb = sbuf(es, "lamb", [128, 256], F32)
        lamt = sbuf(es, "lamt", [128, 128], F32)
        lams = sbuf(es, "lams", [128, 4], F32)
        neglam = sbuf(es, "neglam", [128, 1], F32)
        gsub = sbuf(es, "gsub", [128, 1], F32)

        PS_S = [psum("pss0", [128, 2, 512], F32), psum("pss1", [128, 2, 512], F32)]
        PS_O = psum("pso", [128, 4, 512], F32)

        def sbank(i, b):
            return PS_S[i][:, b, :]

        def sbank16(i, b):
            return PS_S[i][:, b, :].bitcast(BF16)

        def obank(k):
            return PS_O[:, k, :]

        def KS(i, b):
            return ("S", i, b)

        def KO(k):
            return ("O", k)

        S.dma("sp", cs[:], cs_d, writes=["cs"])
        S.dma("sp", css[:], css_d, writes=["css"])
        S.dma("pool", ident[:], ident_d, writes=["ident"])
        S.dma("pool", epat[:], epat_d, writes=["epat"])
        S.op("dve", lambda e: e.memset(ones[:], 1.0), writes=["ones"])
        S.op("dve", lambda e: e.memset(onesf[:], 1.0), writes=["onesf"])
        for t in range(NT):
            S.dma("sp", h[:, t, :], x_d[t * 128:(t + 1) * 128, :], writes=[("h", t)])

        lam0 = 0.8 - 0.6 * math.exp(-0.3 * 0)
        S.dma("sp", lamb[:], alam_d.partition_broadcast(128), writes=["lamb"])
        S.op("dve", lambda e: e.tensor_tensor(lamt[:, 0:64], lamb[:, 0:64], lamb[:, 64:128], ALU.mult),
             reads=["lamb"], writes=["lamt0"])
        S.op("dve", lambda e: e.tensor_tensor(lamt[:, 64:128], lamb[:, 128:192], lamb[:, 192:256], ALU.mult),
             reads=["lamb"], writes=["lamt1"])
        S.op("dve", lambda e: e.reduce_sum(lams[:, 0:1], lamt[:, 0:64], AX.X), reads=["lamt0"], writes=["lams0"])
        S.op("dve", lambda e: e.reduce_sum(lams[:, 1:2], lamt[:, 64:128], AX.X), reads=["lamt1"], writes=["lams1"])
        S.op("act", lambda e: e.activation(out=lams[:, 2:4], in_=lams[:, 0:2], func=AF.Exp),
             reads=["lams0", "lams1"], writes=["lams2"])
        S.op("dve", lambda e: e.tensor_tensor(neglam[:], lams[:, 3:4], lams[:, 2:3], ALU.subtract),
             reads=["lams2"], writes=["neglam"])
        S.op("dve", lambda e: e.tensor_scalar(neglam[:], neglam[:], -lam0, 1.0, ALU.add, ALU.mult),
             reads=["neglam"], writes=["neglam"])
        S.dma("sp", gsub[:], asub_d.rearrange("(p o) -> p o", o=1), writes=["gsub"])
        S.op("dve", lambda e: e.tensor_scalar(gsub[:], gsub[:], 1.0 - lam0, 0.0, ALU.mult, ALU.add),
             reads=["gsub"], writes=["gsub"])

        def relbias_prep(stack):
            ext = sbuf(stack, "ext", [16, LEXT], F32)
            S.dma("sp", ext[:, 0:257], brel_d, writes=["ext"])
            S.op("dve", lambda e: e.tensor_copy(ext[:, 257:LEXT], ext[:, 256:257].broadcast_to([16, LEXT - 257])),
                 reads=["ext"], writes=["ext"])
            rep_v = rep_d.rearrange("(h r m) -> h r m", h=16, r=128)
            S.dma("sp", rep_v, ext[:].unsqueeze(1).broadcast_to([16, 128, LEXT]), reads=["ext"], writes=["rep"])

        def load_w(dst, src2d, key, nsplit=4):
            v = src2d.rearrange("(k p) n -> p k n", p=128)
            step = 8 // nsplit
            for i in range(nsplit):
                S.dma("pool", dst[:, i * step:(i + 1) * step, :], v[:, i * step:(i + 1) * step, :],
                      writes=[(key, i)])
            return [(key, i) for i in range(nsplit)]

        def stats_begin():
            S.op("dve", lambda e: e.memset(ss[:], 0.0), writes=["ss"])

        def stats_tile(t):
            S.op("act", lambda e, t=t, jk=junk: e.activation(out=jk[:], in_=h[:, t, :], func=AF.Square,
                                                             accum_out=ss[:, t:t + 1]),
                 reads=[("h", t), "ss"], writes=["junk", ("ss", t)])

        def stats_end():
            S.op("dve", lambda e: e.tensor_scalar(rstd[:], ss[:], 1.0 / D, EPS, ALU.mult, ALU.add),
                 reads=[("ss", t) for t in range(NT)], writes=["rstd"])
            S.op("act", lambda e: e.activation(out=rstd[:], in_=rstd[:], func=AF.Sqrt), reads=["rstd"], writes=["rstd"])
            S.op("dve", lambda e: e.reciprocal(rstd[:], rstd[:]), reads=["rstd"], writes=["rstd"])

        def norm_stats():
            stats_begin()
            for t in range(NT):
                stats_tile(t)
            stats_end()

        def norm_to_A(g_ap):
            S.dma("sp", grow[:], g_ap.partition_broadcast(128), writes=["grow"])
            for t in range(NT):
                xb = xn[t % 2]
                S.op("dve", lambda e, t=t, xb=xb: e.scalar_tensor_tensor(xb[:], h[:, t, :], rstd[:, t:t + 1], grow[:],
                                                                     ALU.mult, ALU.mult),
                     reads=[("h", t), "rstd", "grow"], writes=[("xn", t % 2)])
                pt = sbank16(t % 2, 0)

                def tr(e, xb=xb, pt=pt):
                    ins = None
                    for kc in range(8):
                        ins = e.transpose(pt[:, kc * 128:(kc + 1) * 128], xb[:, kc * 128:(kc + 1) * 128], ident[:])
                    return ins
                S.op("pe", tr, reads=[("xn", t % 2), "ident"], excl=[KS(t % 2, 0)])
                S.op("act", lambda e, t=t, pt=pt: e.activation(out=A[:, :, t * 128:(t + 1) * 128],
                                                               in_=pt.rearrange("p (k c) -> p k c", k=8), func=AF.Copy),
                     writes=[("A", t)], excl=[KS(t % 2, 0)])

        def Akeys(t0, t1):
            return [("A", t) for t in range(t0, t1)]

        def rope(src4, dst4, cosb, sinb, tc, ts, shape4, rkeys, wkeys, excl, tkey):
            S.op("dve", lambda e: e.tensor_tensor(tc, src4, cosb, ALU.mult), reads=rkeys, writes=[tkey + "c"], excl=excl)
            S.op("dve", lambda e: e.tensor_tensor(ts, src4, sinb, ALU.mult), reads=rkeys, writes=[tkey + "s"], excl=excl)
            S.op("pool", lambda e: e.tensor_tensor(dst4[:, :, 0, :], tc[:, :, 0, :], ts[:, :, 1, :], ALU.subtract),
                 reads=[tkey + "c", tkey + "s"], writes=wkeys)
            S.op("pool", lambda e: e.tensor_tensor(dst4[:, :, 1, :], tc[:, :, 1, :], ts[:, :, 0, :], ALU.add),
                 reads=[tkey + "c", tkey + "s"], writes=wkeys)

        def kv_slots():
            return [(PS_S[0][:, 0, :], KS(0, 0), PS_S[0][:, 1, :], KS(0, 1)),
                    (PS_S[1][:, 0, :], KS(1, 0), PS_S[1][:, 1, :], KS(1, 1)),
                    (obank(0), KO(0), obank(1), KO(1)),
                    (obank(2), KO(2), obank(3), KO(3))]

        def kv_phase(layer, u, Wkv, wkeys, out_k, out_v):
            slots = kv_slots()
            items = [("p", t) for t in range(NT - 1)] + [("s", s) for s in range(4)]

            def stage_a(idx):
                kind, t = items[idx]
                q = idx % 4
                bkv, kkv, btr, ktr = slots[q]
                P = 128 if kind == "p" else 32
                c0 = t * 128 if kind == "p" else SEQ + 32 * t
                pk = bkv[0:P, 0:256]

                def mm(e, c0=c0, P=P, pk=pk):
                    ins = None
                    for kc in range(8):
                        ins = e.matmul(pk, A[:, kc, c0:c0 + P], Wkv[:, kc, :], start=(kc == 0), stop=(kc == 7))
                    return ins
                S.op("pe", mm, reads=[("A", t if kind == "p" else 16)] + wkeys, excl=[kkv])
                kst = kstage[q % len(kstage)][0:P, :]
                qk = q % len(kstage)
                vst = vstage[q][0:P, :]
                k16q = k16[q % len(k16)][0:P, :]
                if layer == 0:
                    if kind == "p":
                        cosb = cs[:, t, 0:32].unsqueeze(1).unsqueeze(1).broadcast_to([128, 2, 2, 32])
                        sinb = cs[:, t, 32:64].unsqueeze(1).unsqueeze(1).broadcast_to([128, 2, 2, 32])
                        ck = "cs"
                    else:
                        cosb = css[:, 0:32].unsqueeze(1).unsqueeze(1).broadcast_to([32, 2, 2, 32])
                        sinb = css[:, 32:64].unsqueeze(1).unsqueeze(1).broadcast_to([32, 2, 2, 32])
                        ck = "css"
                    r4 = "p (g c f) -> p g c f"
                    rope(pk[:, 0:128].rearrange(r4, c=2, f=32), kst.rearrange(r4, c=2, f=32), cosb, sinb,
                         ktc[q][0:P, :].rearrange(r4, c=2, f=32), kts[q][0:P, :].rearrange(r4, c=2, f=32), None,
                         [ck], [("kst", qk)], [kkv], "kt%d" % q)
                else:
                    S.op("act", lambda e, kst=kst, pk=pk: e.activation(out=kst, in_=pk[:, 0:128], func=AF.Copy),
                         writes=[("kst", qk)], excl=[kkv])
                S.op("act", lambda e, vst=vst, pk=pk: e.activation(out=vst, in_=pk[:, 128:256], func=AF.Copy),
                     writes=[("vst", q)], excl=[kkv])
                S.op("act", lambda e, kst=kst, k16q=k16q: e.activation(out=k16q, in_=kst, func=AF.Copy),
                     reads=[("kst", qk)], writes=[("k16", q)])
                vt = V[:, t] if kind == "p" else Vs[:, t]
                if layer == 0:
                    S.op("pool", lambda e, vt=vt, vst=vst: e.tensor_copy(vt[:, 0, :], vst),
                         reads=[("vst", q)], writes=["V" if kind == "p" else "Vs"])
                else:
                    base = vt[:, 0, 0:1]
                    vdst = bass.AP(base.tensor, base.offset, [[base.ap[0][0], P], [192, 2], [1, 64]])
                    S.op("pool", lambda e, vdst=vdst, vst=vst: e.tensor_copy(vdst, vst.rearrange("p (a e) -> p a e", a=2)),
                         reads=[("vst", q)], writes=["V" if kind == "p" else "Vs"])
                if kind == "p":
                    need_out = (layer == 0) or (t >= 12)
                    r0 = t * 128 if layer == 0 else (t - 12) * 128
                else:
                    need_out = True
                    r0 = (SEQ if layer == 0 else 512) + 32 * t
                if need_out:
                    S.dma("sp", out_k[r0:r0 + P, u * 128:(u + 1) * 128], kst, reads=[("kst", qk)])
                    S.dma("sp", out_v[r0:r0 + P, u * 128:(u + 1) * 128], vst, reads=[("vst", q)])

            def stage_b(idx):
                kind, t = items[idx]
                q = idx % 4
                bkv, kkv, btr, ktr = slots[q]
                P = 128 if kind == "p" else 32
                c0 = t * 128 if kind == "p" else SEQ + 32 * t
                k16q = k16[q][0:P, :]
                ptk = btr.bitcast(BF16)
                S.op("pe", lambda e, ptk=ptk, k16q=k16q, P=P: e.transpose(ptk[:, 0:P], k16q, ident[0:P, 0:P]),
                     reads=[("k16", q), "ident"], excl=[ktr])
                S.op("dve", lambda e, c0=c0, P=P, ptk=ptk: e.tensor_copy(KT[:, c0:c0 + P], ptk[:, 0:P]),
                     writes=["KT"], excl=[ktr])

            SKEW = 3
            for idx in range(len(items) + SKEW):
                if idx < len(items):
                    stage_a(idx)
                if idx - SKEW >= 0:
                    stage_b(idx - SKEW)

        def finalize_head(layer, u, ncol, col0, bO, bL, tag, tb):
            outB = B[:, u, col0:col0 + ncol]
            tb = tb % len(T1)
            t1 = T1[tb][:, 0:ncol]
            k1 = ("T1", tb)
            if layer == 0:
                t2 = T2[tb][:, 0:ncol]
                k2 = ("T2", tb)
                o1, o2, l1, l2 = (obank(k)[:, 0:ncol] for k in (0, 1, 2, 3))
                t3 = T3[:, 0:ncol]
                t4 = T4[:, 0:ncol]
                S.op("dve", lambda e: e.tensor_copy(t3, o1), writes=["T3"], excl=[KO(0)])
                S.op("dve", lambda e: e.tensor_copy(t4, o2), writes=["T4"], excl=[KO(1)])
                S.op("act", lambda e: e.activation(out=t1, in_=l1, func=AF.Copy), writes=[k1], excl=[KO(2)])
                S.op("act", lambda e: e.activation(out=t2, in_=l2, func=AF.Copy), writes=[k2], excl=[KO(3)])
                S.op("dve", lambda e: e.tensor_tensor(t3, t3, t2, ALU.mult), reads=[k2, "T3"], writes=["T3"])
                S.op("dve", lambda e: e.tensor_tensor(t4, t4, t1, ALU.mult), reads=[k1, "T4"], writes=["T4"])
                S.op("dve", lambda e: e.scalar_tensor_tensor(t3, t4, neglam[:, 0:1], t3, ALU.mult, ALU.add),
                     reads=["T3", "T4", "neglam"], writes=["T3"])
                S.op("pool", lambda e: e.tensor_tensor(t4, t3, t3, ALU.mult), reads=["T3"], writes=["T4"])
                S.op("dve", lambda e: e.tensor_tensor(t1, t1, t2, ALU.mult), reads=[k1, k2], writes=[k1])
                S.op("dve", lambda e: e.scalar_tensor_tensor(t1, t1, EPS, t1, ALU.mult, ALU.mult), reads=[k1], writes=[k1])

                def tail(psq, kpsq):
                    S.op("pe", lambda e: e.matmul(psq[:, 0:ncol], onesf[:], t4, start=True, stop=True),
                         reads=["T4", "onesf"], excl=[kpsq])
                    S.op("dve", lambda e: e.scalar_tensor_tensor(t2, psq[:, 0:ncol], 1.0 / 128, t1, ALU.mult, ALU.add),
                         reads=[k1], writes=[k2], excl=[kpsq])
                    S.op("act", lambda e: e.activation(out=t2, in_=t2, func=AF.Ln), reads=[k2], writes=[k2])
                    S.op("act", lambda e: e.activation(out=t2, in_=t2, func=AF.Exp, scale=-0.5), reads=[k2], writes=[k2])
                    S.op("dve", lambda e: e.scalar_tensor_tensor(outB, t3, gsub[:, 0:1], t2, ALU.mult, ALU.mult),
                         reads=["T3", k2, "gsub"], writes=[("B", u, tag)])
                return tail
            o = obank(bO)[:, 0:ncol]
            l = obank(bL)[:, 0:ncol]
            t5 = T5[:, 0:ncol]
            S.op("dve", lambda e: e.tensor_copy(t5, o), writes=["T5"], excl=[KO(bO)])
            S.op("act", lambda e: e.activation(out=t1, in_=l, func=AF.Ln), writes=[k1], excl=[KO(bL)])
            S.op("act", lambda e: e.activation(out=t1, in_=t1, func=AF.Exp, scale=-1.0), reads=[k1], writes=[k1])
            S.op("dve", lambda e: e.tensor_tensor(outB, t5, t1, ALU.mult), reads=[k1, "T5"],
                 writes=[("B", u, tag)])
            return None

        def attention_prompt(layer, u):
            steps = []
            for j in range(4):
                if layer == 0:
                    kbs = list(range(0, 4 * j + 4))
                else:
                    kbs = list(range(max(0, 4 * j - 4), 4 * j + 4))
                    first = 4 * j - 2 if j >= 1 else 0
                    kbs.remove(first)
                    kbs.insert(0, first)
                for n, kb in enumerate(kbs):
                    st = dict(j=j, kb=kb, first=(n == 0), last=(n == len(kbs) - 1), diag=False, u0=0)
                    if layer == 0:
                        i = kb - 4 * j
                        st["c0"] = 128 * i if i >= 0 else 0
                        st["c1"] = 512
                        st["diag"] = i >= 0
                    else:
                        u0 = 512 * j - 128 * kb
                        st["u0"] = u0
                        st["c0"] = max(0, -u0)
                        st["c1"] = min(512, GW - u0)
                    steps.append(st)
            for n, st in enumerate(steps):
                st["n"] = n
            if layer == 0:
                sbufs = [(PS_S[0], [KS(0, 0), KS(0, 1)]), (PS_S[1], [KS(1, 0), KS(1, 1)])]
                look = 1
            else:
                sbufs = [(PS_S[0], [KS(0, 0), KS(0, 1)]), (PS_S[1], [KS(1, 0), KS(1, 1)]),
                         (PS_O[:, 2:4, :], [KO(2), KO(3)])]
                look = 2
            nsb = len(sbufs)
            npt = len(PT)

            def emit_qk(st):
                n, j, kb, c0, c1 = st["n"], st["j"], st["kb"], st["c0"], st["c1"]
                ps, pkeys = sbufs[n % nsb]
                pt = PT[n % npt]

                def mm(e):
                    ins = None
                    for a in range(2):
                        ins = e.matmul(ps[:, a, c0:c1], KT[64 * a:64 * a + 64, kb * 128:(kb + 1) * 128],
                                       B[64 * a:64 * a + 64, u, j * 512 + c0:j * 512 + c1], start=True, stop=True)
                    return ins
                S.op("pe", mm, reads=["KT", ("B", u, j)], excl=pkeys)
                for a in range(2):
                    kpt = ("PT", n % npt, a)
                    if layer == 0:
                        S.op("act", lambda e, a=a: e.activation(out=pt[:, a, c0:c1], in_=ps[:, a, c0:c1], func=AF.Exp, scale=0.125),
                             writes=[kpt], excl=[pkeys[a]])
                        if st["diag"]:
                            S.op("dve", lambda e, a=a: e.memset(pt[64:128, a, c0:c0 + 64], 0.0), writes=[kpt])
                    else:
                        u0 = st["u0"]
                        sb_ = SB[n % 2]
                        S.op("dve", lambda e, a=a, sb_=sb_, u0=u0: e.scalar_tensor_tensor(
                            sb_[:, a, c0:c1], ps[:, a, c0:c1], 0.125, G[:, a, u0 + c0:u0 + c1], ALU.mult, ALU.add),
                            reads=["G"], writes=[("SB", n % 2, a)], excl=[pkeys[a]])
                        S.op("act", lambda e, a=a, sb_=sb_: e.activation(out=pt[:, a, c0:c1], in_=sb_[:, a, c0:c1], func=AF.Exp),
                             reads=[("SB", n % 2, a)], writes=[kpt])

            def emit_pv(st):
                n, j, kb, c0, c1 = st["n"], st["j"], st["kb"], st["c0"], st["c1"]
                pt = PT[n % npt]
                first, last = st["first"], st["last"]
                for a in range(2):
                    kpt = ("PT", n % npt, a)
                    if layer == 0:
                        def mm(e, a=a):
                            e.matmul(obank(a)[:, c0:c1], V[:, kb, 0, :], pt[:, a, c0:c1], start=first, stop=last)
                            return e.matmul(obank(2 + a)[:, c0:c1], ones[:], pt[:, a, c0:c1], start=first, stop=last)
                        S.op("pe", mm, reads=[kpt, "V", "ones"], excl=[KO(a), KO(2 + a)])
                    else:
                        def mm(e, a=a):
                            e.matmul(obank(0)[:, c0:c1], V[:, kb, a, :], pt[:, a, c0:c1],
                                     start=(first and a == 0), stop=(last and a == 1))
                            return e.matmul(obank(1)[:, c0:c1], epat[:, a, :], pt[:, a, c0:c1],
                                            start=(first and a == 0), stop=(last and a == 1))
                        S.op("pe", mm, reads=[kpt, "V", "epat"], excl=[KO(0), KO(1)])

            for i in range(min(look, len(steps))):
                emit_qk(steps[i])
            pending = None
            for n, st in enumerate(steps):
                if n + look < len(steps):
                    emit_qk(steps[n + look])
                emit_pv(st)
                if pending is not None and (n >= pending[1] or n == len(steps) - 1):
                    ps, pkeys = sbufs[n % nsb]
                    pending[0](ps[:, 0, :], pkeys[0])
                    pending = None
                if st["last"]:
                    j = st["j"]
                    tail = finalize_head(layer, u, 512, j * 512, 0, 1, j, j % 2)
                    if tail is not None:
                        if n == len(steps) - 1:
                            return tail
                        pending = (tail, n + 3)
            return None

        def sample_load(layer, u, s, part="kv"):
            nblk = 8 if layer == 0 else 4
            ck_d, cv_d = (cak_d, cav_d) if layer == 0 else (cbk_d, cbv_d)
            w = s % 2
            if "k" in part:
                kv = ck_d[s, :, u * 128:(u + 1) * 128].rearrange("(b p) d -> p b d", p=128)
                S.dma("pool", CK[w][:, 0:nblk, :], kv, writes=[("CK", w)])
            if "v" not in part:
                return
            if layer == 0:
                vv = cv_d[s, :, u * 128:(u + 1) * 128].rearrange("(b p) d -> p b d", p=128)
                S.dma("pool", CV[w][:, 0:nblk, 0, :], vv, writes=[("CV", w)])
            else:
                for a in range(2):
                    vv = cv_d[s, :, u * 128 + 64 * a:u * 128 + 64 * a + 64].rearrange("(b p) d -> p b d", p=128)
                    S.dma("pool", CV[w][:, 0:nblk, a, 64 * a:64 * a + 64], vv, writes=[("CV", w)])

        def attention_sample(layer, u, last_tail):
            nblk = 8 if layer == 0 else 4
            ncs = nblk * 32

            def st1(s):
                w = s % 2
                pb = s % 2
                ck, ckt = CK[w], CKT[w]
                ptk = sbank16(pb, 0)

                def tr(e):
                    ins = None
                    for b in range(nblk):
                        ins = e.transpose(ptk[:, b * 128:(b + 1) * 128], ck[:, b, :], ident[:])
                    return ins
                S.op("pe", tr, reads=[("CK", w), "ident"], excl=[KS(pb, 0)])
                if s + 2 < 4:
                    sample_load(layer, u, s + 2, "k")
                S.op("act", lambda e: e.activation(out=ckt[:, 0:nblk * 128].rearrange("p (k c) -> p k c", c=128),
                                                   in_=ptk[:, 0:nblk * 128].rearrange("p (k c) -> p k c", c=128), func=AF.Copy),
                     writes=[("CKT", w)], excl=[KS(pb, 0)])

            def st2(s):
                w = s % 2
                pb = s % 2
                c0 = SEQ + 32 * s
                ckt = CKT[w]
                bS = [sbank(pb, 1), sbank(pb, 0)]

                def mmS(e):
                    ins = None
                    for a in range(2):
                        for b in range(nblk):
                            slot = b if layer == 0 else nblk - 1 - b
                            ins = e.matmul(bS[a][:, slot * 32:slot * 32 + 32],
                                           ckt[64 * a:64 * a + 64, b * 128:(b + 1) * 128],
                                           B[64 * a:64 * a + 64, u, c0:c0 + 32], start=True, stop=True)
                        ins = e.matmul(bS[a][0:32, 256:288], KT[64 * a:64 * a + 64, c0:c0 + 32],
                                       B[64 * a:64 * a + 64, u, c0:c0 + 32], start=True, stop=True)
                    return ins
                S.op("pe", mmS, reads=[("CKT", w), "KT", ("B", u, 4)], excl=[KS(pb, 0), KS(pb, 1)])
                pts, ptn = PTs[w], PTn[w]
                for a in range(2):
                    ka = KS(pb, 1 - a)
                    if layer == 0:
                        S.op("act", lambda e, a=a: e.activation(out=pts[:, a, 0:ncs], in_=bS[a][:, 0:ncs], func=AF.Exp, scale=0.125),
                             writes=[("PTs", w)], excl=[ka])
                        S.op("act", lambda e, a=a: e.activation(out=ptn[:, a, :], in_=bS[a][0:32, 256:288], func=AF.Exp, scale=0.125),
                             writes=[("PTn", w)], excl=[ka])
                    else:
                        sb_ = SB[w]
                        gs = G[:, a, 128:640].rearrange("p (s x) -> p s x", x=128)[:, :, 0:32]
                        S.op("dve", lambda e, a=a, gs=gs, sb_=sb_: e.scalar_tensor_tensor(
                            sb_[:, a, 0:ncs].rearrange("p (s x) -> p s x", x=32),
                            bS[a][:, 0:ncs].rearrange("p (s x) -> p s x", x=32), 0.125, gs, ALU.mult, ALU.add),
                            reads=["G"], writes=[("SB", w)], excl=[ka])
                        S.op("dve", lambda e, a=a, sb_=sb_: e.scalar_tensor_tensor(
                            sb_[0:32, a, 256:288], bS[a][0:32, 256:288], 0.125, G[0:32, a, 0:32], ALU.mult, ALU.add),
                            reads=["G"], writes=[("SB", w)], excl=[ka])
                        S.op("act", lambda e, a=a, sb_=sb_: e.activation(out=pts[:, a, 0:ncs], in_=sb_[:, a, 0:ncs], func=AF.Exp),
                             reads=[("SB", w)], writes=[("PTs", w)])
                        S.op("act", lambda e, a=a, sb_=sb_: e.activation(out=ptn[:, a, :], in_=sb_[0:32, a, 256:288], func=AF.Exp),
                             reads=[("SB", w)], writes=[("PTn", w)])

            def st3(s):
                w = s % 2
                cv, pts, ptn = CV[w], PTs[w], PTn[w]

                def mmPV(e):
                    ins = None
                    cs0 = s * 32
                    for b in range(nblk):
                        slot = b if layer == 0 else nblk - 1 - b
                        for a in range(2):
                            rhs = pts[:, a, slot * 32:slot * 32 + 32]
                            if layer == 0:
                                e.matmul(obank(a)[:, cs0:cs0 + 32], cv[:, b, 0, :], rhs, start=(b == 0), stop=False)
                                ins = e.matmul(obank(2 + a)[:, cs0:cs0 + 32], ones[:], rhs, start=(b == 0), stop=False)
                            else:
                                st0 = (b == 0 and a == 0)
                                e.matmul(obank(0)[:, cs0:cs0 + 32], cv[:, b, a, :], rhs, start=st0, stop=False)
                                ins = e.matmul(obank(1)[:, cs0:cs0 + 32], epat[:, a, :], rhs, start=st0, stop=False)
                    for a in range(2):
                        rhs = ptn[0:32, a, :]
                        if layer == 0:
                            e.matmul(obank(a)[:, cs0:cs0 + 32], Vs[0:32, s, 0, :], rhs, start=False, stop=True)
                            ins = e.matmul(obank(2 + a)[:, cs0:cs0 + 32], ones[0:32, :], rhs, start=False, stop=True)
                        else:
                            e.matmul(obank(0)[:, cs0:cs0 + 32], Vs[0:32, s, a, :], rhs, start=False, stop=(a == 1))
                            ins = e.matmul(obank(1)[:, cs0:cs0 + 32], epat[0:32, a, :], rhs, start=False, stop=(a == 1))
                    return ins
                ex = [KO(0), KO(1), KO(2), KO(3)] if layer == 0 else [KO(0), KO(1)]
                S.op("pe", mmPV, reads=[("PTs", w), ("PTn", w), ("CV", w), "Vs", "ones", "epat"], excl=ex)
                if s + 2 < 4:
                    sample_load(layer, u, s + 2, "v")

            st1(0)
            st1(1)
            st2(0)
            if last_tail is not None:
                last_tail(sbank(1, 1), KS(1, 1))
            st2(1); st3(0); st1(2); st2(2); st3(1); st1(3); st2(3); st3(2); st3(3)
            tail = finalize_head(layer, u, 128, SEQ, 0, 1, 4, 0)
            if tail is not None:
                tail(sbank(0, 0), KS(0, 0))

        def out_proj(W, wkeys):
            stats_begin()
            for t in range(NT):
                pb = t % 2
                ps = PS_S[pb]
                jt = min(t // 4, 4)

                def mm(e, t=t, ps=ps):
                    ins = None
                    for nh in range(2):
                        for kc in range(8):
                            ins = e.matmul(ps[:, nh, :], B[:, kc, t * 128:(t + 1) * 128], W[:, kc, nh * 512:(nh + 1) * 512],
                                           start=(kc == 0), stop=(kc == 7))
                    return ins
                S.op("pe", mm, reads=[("B", kc, jt) for kc in range(8)] + wkeys, excl=[KS(pb, 0), KS(pb, 1)])
                S.op("dve", lambda e, t=t, ps=ps: e.tensor_tensor(h[:, t, :], ps[:].rearrange("p a n -> p (a n)"), h[:, t, :], ALU.add),
                     reads=[("h", t)], writes=[("h", t)], excl=[KS(pb, 0), KS(pb, 1)])
                stats_tile(t)
            stats_end()

        def mlp(l):
            norm_to_A(g_mlp_d[l])
            if l == 0:
                relbias_prep(es_m)
            stats_begin()
            blocks = [(0, 512), (512, 512), (1024, 512), (1536, 512), (2048, 128)]
            wkeys = {}

            def load_group(fg):
                k1 = load_w(W1b[fg % 2], wff1_d[l][:, fg * 1024:(fg + 1) * 1024], ("W1", fg % 2))
                k2 = load_w(W2b[fg % 2], wff2_d[l][fg * 1024:(fg + 1) * 1024, :], ("W2", fg % 2))
                wkeys[fg] = (k1, k2)

            def up(fg, bi, seq):
                W1 = W1b[fg % 2]
                k1 = wkeys[fg][0]
                c0, n = blocks[bi]
                Hb = H1[seq % 2]
                for fc in range(8):
                    hb = fc % 4
                    ph = obank(hb)[:, 0:n]

                    def mm(e, fc=fc, ph=ph):
                        ins = None
                        for kc in range(8):
                            ins = e.matmul(ph, W1[:, kc, fc * 128:(fc + 1) * 128], A[:, kc, c0:c0 + n],
                                           start=(kc == 0), stop=(kc == 7))
                        return ins
                    S.op("pe", mm, reads=Akeys(c0 // 128, (c0 + n) // 128) + k1, excl=[KO(hb)])
                    rl = RL[fc % 2]
                    S.op("act", lambda e, ph=ph, rl=rl: e.activation(out=rl[:, 0:n], in_=ph, func=AF.Relu),
                         writes=[("RL", fc % 2)], excl=[KO(hb)])
                    S.op("pool", lambda e, fc=fc, rl=rl: e.tensor_tensor(Hb[:, fc, 0:n], rl[:, 0:n], rl[:, 0:n], ALU.mult),
                         reads=[("RL", fc % 2)], writes=[("H1", seq % 2, fc)])

            def down(fg, bi, seq):
                W2 = W2b[fg % 2]
                k2 = wkeys[fg][1]
                c0, n = blocks[bi]
                Hb = H1[seq % 2]
                for tt in range(n // 128):
                    t = c0 // 128 + tt
                    pb = t % 2
                    ps = PS_S[pb]

                    def mm2(e, tt=tt, ps=ps):
                        ins = None
                        for nh in range(2):
                            for fc in range(8):
                                ins = e.matmul(ps[:, nh, :], Hb[:, fc, tt * 128:(tt + 1) * 128],
                                               W2[:, fc, nh * 512:(nh + 1) * 512], start=(fc == 0), stop=(fc == 7))
                        return ins
                    S.op("pe", mm2, reads=[("H1", seq % 2, fc) for fc in range(8)] + k2, excl=[KS(pb, 0), KS(pb, 1)])
                    S.op("dve", lambda e, t=t, ps=ps: e.tensor_tensor(h[:, t, :], ps[:].rearrange("p a n -> p (a n)"),
                                                                  h[:, t, :], ALU.add),
                         reads=[("h", t)], writes=[("h", t)], excl=[KS(pb, 0), KS(pb, 1)])
                    if fg == 3:
                        stats_tile(t)

            work = [(fg, bi) for fg in range(4) for bi in range(len(blocks))]
            load_group(0)
            for i, (fg, bi) in enumerate(work):
                if bi == 1 and fg + 1 < 4:
                    load_group(fg + 1)
                if i == 0:
                    up(fg, bi, i)
                if i + 1 < len(work):
                    up(work[i + 1][0], work[i + 1][1], i + 1)
                down(fg, bi, i)
            stats_end()

        for layer in range(2):
            with ExitStack() as es_l:
                B = sbuf(es_l, "B%d" % layer, [128, 8, NTOK], BF16)
                with ExitStack() as es_q:
                    grow = sbuf(es_q, "grow", [128, D], F32)
                    xn = [sbuf(es_q, "xn0", [128, D], BF16), sbuf(es_q, "xn1", [128, D], BF16)]
                    junk = sbuf(es_q, "junk", [128, D], BF16)
                    Wq = sbuf(es_q, "Wq", [128, 8, D], BF16)
                    if layer == 0:
                        norm_stats()
                        tcq = [sbuf(es_q, "tcq%d" % i, [128, D], F32) for i in range(2)]
                        tsq = [sbuf(es_q, "tsq%d" % i, [128, D], F32) for i in range(2)]
                        qr = [sbuf(es_q, "qr%d" % i, [128, D], BF16) for i in range(2)]
                        wk = load_w(Wq, wqkv_d[:, 0:D], "Wq")
                        norm_to_A(g_attn_d[0])
                        r4 = "p (g c f) -> p g c f"

                        def q_a(t):
                            pb = t % 2
                            ps = PS_S[pb]

                            def mm(e):
                                ins = None
                                for nh in range(2):
                                    for kc in range(8):
                                        ins = e.matmul(ps[:, nh, :], A[:, kc, t * 128:(t + 1) * 128],
                                                       Wq[:, kc, nh * 512:(nh + 1) * 512], start=(kc == 0), stop=(kc == 7))
                                return ins
                            S.op("pe", mm, reads=[("A", t)] + wk, excl=[KS(pb, 0), KS(pb, 1)])
                            cosb = cs[:, t, 0:32].unsqueeze(1).unsqueeze(1).broadcast_to([128, 16, 2, 32])
                            sinb = cs[:, t, 32:64].unsqueeze(1).unsqueeze(1).broadcast_to([128, 16, 2, 32])
                            src4 = ps[:].rearrange("p a (g c f) -> p (a g) c f", c=2, f=32)
                            tc = tcq[pb][:].rearrange(r4, c=2, f=32)
                            ts = tsq[pb][:].rearrange(r4, c=2, f=32)
                            dst4 = qr[pb][:].rearrange(r4, c=2, f=32)
                            ex = [KS(pb, 0), KS(pb, 1)]
                            S.op("dve", lambda e: e.tensor_tensor(tc, src4, cosb, ALU.mult), reads=["cs"], writes=[("tqc", pb)], excl=ex)
                            S.op("dve", lambda e: e.tensor_tensor(ts, src4, sinb, ALU.mult), reads=["cs"], writes=[("tqs", pb)], excl=ex)
                            S.op("pool", lambda e: e.tensor_tensor(dst4[:, :, 0, :], tc[:, :, 0, :], ts[:, :, 1, :], ALU.subtract),
                                 reads=[("tqc", pb), ("tqs", pb)], writes=[("qr0", pb)])
                            S.op("dve", lambda e: e.tensor_tensor(dst4[:, :, 1, :], tc[:, :, 1, :], ts[:, :, 0, :], ALU.add),
                                 reads=[("tqc", pb), ("tqs", pb)], writes=[("qr1", pb)])

                        def q_b(t):
                            pb = t % 2
                            pt = obank(t % 4).bitcast(BF16)
                            qrt = qr[pb]

                            def tr(e):
                                ins = None
                                for kc in range(8):
                                    ins = e.transpose(pt[:, kc * 128:(kc + 1) * 128], qrt[:, kc * 128:(kc + 1) * 128], ident[:])
                                return ins
                            S.op("pe", tr, reads=[("qr0", pb), ("qr1", pb), "ident"], excl=[KO(t % 4)])
                            jt = min(t // 4, 4)
                            S.op("act", lambda e: e.activation(out=B[:, :, t * 128:(t + 1) * 128],
                                                               in_=pt.rearrange("p (k c) -> p k c", k=8), func=AF.Copy),
                                 writes=[("B", kc, jt) for kc in range(8)], excl=[KO(t % 4)])
                        for t in range(NT + 1):
                            if t < NT:
                                q_a(t)
                            if t >= 1:
                                q_b(t - 1)
                    else:
                        wk = load_w(Wq, wbq_d, "Wq")
                        norm_to_A(g_attn_d[1])
                        blocks = [(0, 512), (512, 512), (1024, 512), (1536, 512), (2048, 128)]
                        n_ = 0
                        for c in range(8):
                            for bi, (c0, n) in enumerate(blocks):
                                pk = obank(n_ % 4)[:, 0:n]

                                def mm(e, c=c, c0=c0, n=n, pk=pk):
                                    ins = None
                                    for kc in range(8):
                                        ins = e.matmul(pk, Wq[:, kc, c * 128:(c + 1) * 128], A[:, kc, c0:c0 + n],
                                                       start=(kc == 0), stop=(kc == 7))
                                    return ins
                                S.op("pe", mm, reads=Akeys(c0 // 128, (c0 + n) // 128) + wk, excl=[KO(n_ % 4)])
                                S.op("act", lambda e, c=c, c0=c0, n=n, pk=pk: e.activation(out=B[:, c, c0:c0 + n], in_=pk, func=AF.Copy),
                                     writes=[("B", c, bi)], excl=[KO(n_ % 4)])
                                n_ += 1
                        norm_to_A(g_kv_d)
                    S.end_phase()
                with ExitStack() as es_a:
                    KT = sbuf(es_a, "KT", [128, NTOK], BF16)
                    V = sbuf(es_a, "V", [128, 16, 2, 128], BF16)
                    Vs = sbuf(es_a, "Vs", [32, 4, 2, 128], BF16)
                    nblk_l = 8 if layer == 0 else 4
                    npt_l = 2 if layer == 0 else 3
                    PT = [sbuf(es_a, "PT%d" % i, [128, 2, 512], BF16) for i in range(npt_l)]
                    T1 = [sbuf(es_a, "T1a", [128, 512], F32)]
                    if layer == 0:
                        T1.append(sbuf(es_a, "T1b", [128, 512], F32))
                    if layer == 0:
                        T2 = [sbuf(es_a, "T2a", [128, 512], F32), sbuf(es_a, "T2b", [128, 512], F32)]
                        T3 = sbuf(es_a, "T3", [128, 512], F32)
                        T4 = sbuf(es_a, "T4", [128, 512], F32)
                    CK = [sbuf(es_a, "CK%d" % i, [128, nblk_l, 128], BF16) for i in range(2)]
                    CV = [sbuf(es_a, "CV%d" % i, [128, nblk_l, 2 if layer == 1 else 1, 128], BF16) for i in range(2)]
                    CKT = [sbuf(es_a, "CKT%d" % i, [128, nblk_l * 128], BF16) for i in range(2)]
                    PTs = [sbuf(es_a, "PTs%d" % i, [128, 2, 256], BF16) for i in range(2)]
                    PTn = [sbuf(es_a, "PTn%d" % i, [32, 2, 32], BF16) for i in range(2)]
                    Wkvb = [sbuf(es_a, "Wkv0", [128, 8, 256], BF16), sbuf(es_a, "Wkv1", [128, 8, 256], BF16)]
                    kstage = [sbuf(es_a, "kst%d" % i, [128, 128], F32) for i in range(4 if layer == 0 else 2)]
                    vstage = [sbuf(es_a, "vst%d" % i, [128, 128], F32) for i in range(4)]
                    k16 = [sbuf(es_a, "k16_%d" % i, [128, 128], BF16) for i in range(4)]
                    if layer == 0:
                        ktc = [sbuf(es_a, "ktc%d" % i, [128, 128], F32) for i in range(4)]
                        kts = [sbuf(es_a, "kts%d" % i, [128, 128], F32) for i in range(4)]
                    if layer == 1:
                        T5 = sbuf(es_a, "T5", [128, 512], F32)
                        SB = [sbuf(es_a, "SB%d" % i, [128, 2, 512], F32) for i in range(2)]
                        G = sbuf(es_a, "G", [128, 2, GW], F32)
                        Mk = sbuf(es_a, "Mk", [128, GW], F32)
                        S.dma("sp", Mk[:], mask_d, writes=["Mk"])
                        S.op("pool", lambda e: e.memset(V[:], 0.0), writes=["V"])
                        S.op("pool", lambda e: e.memset(Vs[:], 0.0), writes=["Vs"])
                        for i in range(2):
                            S.op("pool", lambda e, i=i: e.memset(CV[i][:], 0.0), writes=[("CV", i)])
                    out_k, out_v = (ak_d, av_d) if layer == 0 else (bk_d, bv_d)
                    def load_wkv(u):
                        Wkv = Wkvb[u % 2]
                        if layer == 0:
                            srcs = [wqkv_d[:, D + u * 128:D + (u + 1) * 128], wqkv_d[:, 2 * D + u * 128:2 * D + (u + 1) * 128]]
                        else:
                            srcs = [wkv_d[:, u * 128:(u + 1) * 128], wkv_d[:, D + u * 128:D + (u + 1) * 128]]
                        for i, src in enumerate(srcs):
                            S.dma("pool", Wkv[:, :, i * 128:(i + 1) * 128], src.rearrange("(k p) n -> p k n", p=128),
                                  writes=[("Wkv", u % 2, i)])
                    load_wkv(0)
                    for u in range(8):
                        Wkv = Wkvb[u % 2]
                        wkeys = [("Wkv", u % 2, 0), ("Wkv", u % 2, 1)]
                        if layer == 1:
                            for a in range(2):
                                src = bass.AP(rep_d.tensor, (2 * u + a) * 128 * LEXT + 128, [[LEXT - 1, 128], [1, GW]])
                                S.dma("sp", G[:, a, :], src, reads=["rep"], writes=["G"])
                            S.op("pool", lambda e: e.tensor_tensor(G[:], G[:], Mk[:].unsqueeze(1).broadcast_to([128, 2, GW]), ALU.add),
                                 reads=["G", "Mk"], writes=["G"])
                        kv_phase(layer, u, Wkv, wkeys, out_k, out_v)
                        if u + 1 < 8:
                            load_wkv(u + 1)
                        sample_load(layer, u, 0)
                        sample_load(layer, u, 1)
                        last_tail = attention_prompt(layer, u)
                        attention_sample(layer, u, last_tail)
                    S.end_phase()
                with ExitStack() as es_o:
                    Wo = sbuf(es_o, "Wo", [128, 8, D], BF16)
                    junk = sbuf(es_o, "junko", [128, D], BF16)
                    wk = load_w(Wo, wao_d if layer == 0 else wbo_d, "Wo")
                    out_proj(Wo, wk)
                    S.end_phase()
            with ExitStack() as es_m:
                grow = sbuf(es_m, "growm", [128, D], F32)
                xn = [sbuf(es_m, "xnm0", [128, D], BF16), sbuf(es_m, "xnm1", [128, D], BF16)]
                junk = sbuf(es_m, "junkm", [128, D], BF16)
                W1b = [sbuf(es_m, "W1a", [128, 8, D], BF16), sbuf(es_m, "W1b", [128, 8, D], BF16)]
                W2b = [sbuf(es_m, "W2a", [128, 8, D], BF16), sbuf(es_m, "W2b", [128, 8, D], BF16)]
                H1 = [sbuf(es_m, "H1a", [128, 8, 512], BF16), sbuf(es_m, "H1b", [128, 8, 512], BF16)]
                RL = [sbuf(es_m, "RL0", [128, 512], F32), sbuf(es_m, "RL1", [128, 512], F32)]
                mlp(layer)
                S.end_phase()
        with ExitStack() as es_f:
            grow = sbuf(es_f, "growf", [128, D], F32)
            junk = sbuf(es_f, "junkf", [128, D], BF16)
            ys = [sbuf(es_f, "ys%d" % i, [128, D], F32) for i in range(3)]
            S.dma("sp", grow[:], g_fin_d.partition_broadcast(128), writes=["grow"])
            for t in range(NT):
                yb = ys[t % 3]
                S.op("dve",
                     lambda e, t=t, yb=yb: e.scalar_tensor_tensor(yb[:], h[:, t, :], rstd[:, t:t + 1], grow[:],
                                                                  ALU.mult, ALU.mult),
                     reads=[("h", t), "rstd", "grow"], writes=[("ys", t % 3)])
                S.dma("sp", y_d[t * 128:(t + 1) * 128, :], yb[:], reads=[("ys", t % 3)])
            S.end_phase()
    return nc


def _const_tables():
    half = 32
    inv = 1.0 / (10000.0 ** (np.arange(half, dtype=np.float32) / half))
    pos = np.zeros((128, NT), np.float32)
    for t in range(16):
        pos[:, t] = t * 128 + np.arange(128)
    pos[:, 16] = PAST + (np.arange(128) % 32)
    ang = pos[:, :, None] * inv[None, None, :]
    cs = np.concatenate([np.cos(ang), np.sin(ang)], axis=-1).astype(np.float32)
    angs = (PAST + np.arange(32, dtype=np.float32))[:, None] * inv[None, :]
    css = np.concatenate([np.cos(angs), np.sin(angs)], axis=-1).astype(np.float32)
    ident = np.eye(128, dtype=np.float32)
    kl = np.arange(128)[:, None] // 64
    uu = np.arange(GW)[None, :] // 64
    dlt = uu - kl
    mask = np.where((dlt >= 0) & (dlt <= 8), 0.0, NEG).astype(np.float32)
    epat = np.zeros((128, 2, 128), np.float32)
    epat[:, 0, 0:64] = 1.0
    epat[:, 1, 64:128] = 1.0
    return cs, css, ident, mask, epat


_NC_CACHE = {}


def kernel(x_prompt, x_sample, cache_a_k, cache_a_v, cache_b_k, cache_b_v,
           g_attn, w_a_qkv, a_lambda, a_subln, w_a_o, g_kv, w_kv, w_b_q, b_rel, w_b_o,
           g_mlp, w_ff1, w_ff2, g_final):
    f = lambda a: np.ascontiguousarray(np.asarray(a, dtype=np.float32))
    x_prompt, x_sample = f(x_prompt), f(x_sample)
    cache_a_k, cache_a_v, cache_b_k, cache_b_v = f(cache_a_k), f(cache_a_v), f(cache_b_k), f(cache_b_v)
    cs, css, ident, mask, epat = _const_tables()
    shared = {
        "g_attn": f(g_attn), "w_a_qkv": f(w_a_qkv)[0], "a_lambda": f(a_lambda).reshape(256),
        "a_subln": f(a_subln).reshape(128), "w_a_o": f(w_a_o)[0], "g_kv": f(g_kv), "w_kv": f(w_kv),
        "w_b_q": f(w_b_q)[0], "b_rel": f(b_rel)[0], "w_b_o": f(w_b_o)[0], "g_mlp": f(g_mlp),
        "w_ff1": f(w_ff1), "w_ff2": f(w_ff2), "g_final": f(g_final),
        "c_cs": cs, "c_css": css, "c_ident": ident, "c_mask": mask, "c_epat": epat,
    }
    in_maps = []
    for c in range(N_CORES):
        m = dict(shared)
        m["x"] = np.concatenate([x_prompt[c], x_sample[4 * c:4 * c + 4].reshape(128, D)], axis=0)
        m["cak"] = cache_a_k[0, 4 * c:4 * c + 4].reshape(4, PAST, D)
        m["cav"] = cache_a_v[0, 4 * c:4 * c + 4].reshape(4, PAST, D)
        m["cbk"] = cache_b_k[4 * c:4 * c + 4].reshape(4, 512, D)
        m["cbv"] = cache_b_v[4 * c:4 * c + 4].reshape(4, 512, D)
        in_maps.append(m)
    if "nc" not in _NC_CACHE:
        _NC_CACHE["nc"] = build_nc()
    res = run_bass_kernel_spmd(_NC_CACHE["nc"], in_maps, core_ids=list(range(N_CORES)))
    R = res.results
    y_p = np.stack([R[c]["y"][:SEQ] for c in range(N_CORES)])
    y_s = np.concatenate([R[c]["y"][SEQ:].reshape(4, 32, D) for c in range(N_CORES)])
    ak_p = np.stack([R[c]["ak"][:SEQ].reshape(SEQ, 8, 128) for c in range(N_CORES)])[None]
    av_p = np.stack([R[c]["av"][:SEQ].reshape(SEQ, 8, 128) for c in range(N_CORES)])[None]
    ak_s = np.concatenate([R[c]["ak"][SEQ:].reshape(4, 32, 8, 128) for c in range(N_CORES)])[None]
    av_s = np.concatenate([R[c]["av"][SEQ:].reshape(4, 32, 8, 128) for c in range(N_CORES)])[None]
    bk_p = np.stack([R[c]["bk"][:512].reshape(512, 16, 64) for c in range(N_CORES)])
    bv_p = np.stack([R[c]["bv"][:512].reshape(512, 16, 64) for c in range(N_CORES)])
    bk_s = np.concatenate([R[c]["bk"][512:].reshape(4, 32, 16, 64) for c in range(N_CORES)])
    bv_s = np.concatenate([R[c]["bv"][512:].reshape(4, 32, 16, 64) for c in range(N_CORES)])
    outs = (y_p, y_s, ak_p, av_p, bk_p, bv_p, ak_s, av_s, bk_s, bv_s)
    return tuple(np.ascontiguousarray(o, dtype=np.float32) for o in outs)
```
